# Optimizing a Trainium2 kernel written in Bass

```python
import math
import jax, jax.numpy as jnp
from jax import lax
import numpy as np

D_MODEL = 2048
BATCH = 8
SEQ = 4096
DEPTH = 4
DEC_BATCH = 16
DEC_SEQ = 16
PAST_LEN = 2048

CHUNK = 64
N_MIXERS = 2
N_SSD_LAYERS = (DEPTH + 1) // 2
N_ATT_LAYERS = DEPTH // 2

SSD_EXPAND = 2
D_INNER = SSD_EXPAND * D_MODEL
SSD_HEAD_DIM = 64
SSD_HEADS = D_INNER // SSD_HEAD_DIM
SSD_GROUPS = 8
SSD_HEADS_PER_GROUP = SSD_HEADS // SSD_GROUPS
D_STATE = 128
D_CONV = 4
CONV_DIM = D_INNER + 2 * SSD_GROUPS * D_STATE
SSD_IN_DIM = D_INNER + CONV_DIM + SSD_HEADS
SSD_BLOCK = CHUNK

ATT_HEADS = 8
ATT_HEAD_DIM = D_MODEL // (2 * ATT_HEADS)
ATT_QK_DIM = ATT_HEADS * 2 * ATT_HEAD_DIM
ATT_V_DIM = ATT_HEADS * 2 * ATT_HEAD_DIM
Q_BLOCK = 128

D_FF = 4 * D_MODEL
EPS = 1e-5

kernel_name = 'hybrid_ssd_diffattn_stream_step'


def rmsnorm(x, w):
    xf = x.astype(jnp.float32)
    y = xf * lax.rsqrt(jnp.mean(xf * xf, axis=-1, keepdims=True) + EPS)
    return (y * w.astype(jnp.float32)).astype(x.dtype)


def sqrelu_mlp(x, w_up, w_down):
    h = jax.nn.relu(x @ w_up)
    return (h * h) @ w_down


def causal_conv(xbc, conv_state, w, b):
    xp = jnp.concatenate([conv_state.astype(xbc.dtype), xbc], axis=1)
    y = lax.conv_general_dilated(
        xp, w[:, None, :].astype(xp.dtype), window_strides=(1,), padding='VALID',
        dimension_numbers=('NWC', 'WIO', 'NWC'), feature_group_count=xp.shape[-1])
    return y + b.astype(y.dtype), xp[:, -(D_CONV - 1):]


def ssd_scan(x, dt, a, bm, cm, h0):
    bsz, seq = x.shape[:2]
    blk = min(SSD_BLOCK, seq)
    nc = seq // blk

    def to_blocks(t):
        return jnp.moveaxis(t.reshape((bsz, nc, blk) + t.shape[2:]), 1, 0)

    xs = (to_blocks(x * dt[..., None]), to_blocks(a), to_blocks(bm), to_blocks(cm))
    causal = jnp.tril(jnp.ones((blk, blk), dtype=bool))

    def step(h, inp):
        xdt, ac, bc, cc = inp
        acum = jnp.cumsum(ac, axis=1)
        seg = acum[:, :, None] - acum[:, None, :]
        decay = jnp.exp(jnp.where(causal[None, :, :, None, None], seg, -jnp.inf))
        cb = jnp.einsum('blgn,bsgn->blsg', cc, bc)
        y = jnp.einsum('blsg,blsgr,bsgrp->blgrp', cb, decay, xdt)
        y = y + jnp.einsum('blgn,bgrpn->blgrp', cc, h) * jnp.exp(acum)[..., None]
        tail = jnp.exp(acum[:, -1:] - acum)
        h = h * jnp.exp(acum[:, -1])[..., None, None] + jnp.einsum(
            'blgn,blgrp->bgrpn', bc, xdt * tail[..., None])
        return h, y

    h, ys = lax.scan(step, h0, xs)
    return jnp.moveaxis(ys, 0, 1).reshape(x.shape), h


def ssd_mixer(x, conv_state, ssm_state, w_in, conv_w, conv_b, dt_bias, a_log, d_skip, norm_w, w_out):
    f32 = jnp.float32
    bsz, seq, _ = x.shape
    zxbcdt = x @ w_in
    z, xbc, dt = jnp.split(zxbcdt, [D_INNER, D_INNER + CONV_DIM], axis=-1)
    xbc, new_conv = causal_conv(xbc, conv_state, conv_w, conv_b)
    xbc = jax.nn.silu(xbc)
    xs, bm, cm = jnp.split(xbc, [D_INNER, D_INNER + SSD_GROUPS * D_STATE], axis=-1)
    xs = xs.astype(f32).reshape(bsz, seq, SSD_GROUPS, SSD_HEADS_PER_GROUP, SSD_HEAD_DIM)
    bm = bm.astype(f32).reshape(bsz, seq, SSD_GROUPS, D_STATE)
    cm = cm.astype(f32).reshape(bsz, seq, SSD_GROUPS, D_STATE)
    dt = jax.nn.softplus(dt.astype(f32) + dt_bias.astype(f32))
    dt = dt.reshape(bsz, seq, SSD_GROUPS, SSD_HEADS_PER_GROUP)
    a_neg = -jnp.exp(a_log.astype(f32)).reshape(SSD_GROUPS, SSD_HEADS_PER_GROUP)
    h0 = ssm_state.astype(f32).reshape(bsz, SSD_GROUPS, SSD_HEADS_PER_GROUP, SSD_HEAD_DIM, D_STATE)
    y, h = ssd_scan(xs, dt, dt * a_neg, bm, cm, h0)
    y = y + xs * d_skip.astype(f32).reshape(SSD_GROUPS, SSD_HEADS_PER_GROUP, 1)
    g = (y.reshape(bsz, seq, D_INNER) * jax.nn.silu(z.astype(f32)))
    g = g.reshape(bsz, seq, SSD_GROUPS, D_INNER // SSD_GROUPS)
    g = g * lax.rsqrt(jnp.mean(g * g, axis=-1, keepdims=True) + EPS)
    g = g.reshape(bsz, seq, D_INNER) * norm_w.astype(f32)
    out = g.astype(x.dtype) @ w_out
    return out, new_conv, h.reshape(bsz, SSD_HEADS, SSD_HEAD_DIM, D_STATE)


def diff_attend(q, k, v, q_pos, k_pos, lam):
    f32 = jnp.float32
    s = jnp.einsum('bqhmd,bkhmd->bhmqk', q.astype(f32), k.astype(f32)) * (ATT_HEAD_DIM ** -0.5)
    slopes = 2.0 ** (-8.0 * jnp.arange(1, ATT_HEADS + 1, dtype=f32) / ATT_HEADS)
    dist = jnp.abs(q_pos[:, None] - k_pos[None, :]).astype(f32)
    visible = (k_pos[None, :] // CHUNK) <= (q_pos[:, None] // CHUNK)
    s = s - slopes[:, None, None, None] * dist
    s = jnp.where(visible, s, -jnp.inf)
    p = jax.nn.softmax(s, axis=-1)
    w = p[:, :, 0] - lam * p[:, :, 1]
    return jnp.einsum('bhqk,bkhe->bqhe', w, v.astype(f32))


def diff_attn_mixer(x, cache_k, cache_v, w_qkv, lam_q, lam_k, subln_w, w_o, layer_idx):
    f32 = jnp.float32
    bsz, seq, _ = x.shape
    qkv = x @ w_qkv
    q, k, v = jnp.split(qkv, [ATT_QK_DIM, 2 * ATT_QK_DIM], axis=-1)
    q = q.reshape(bsz, seq, ATT_HEADS, 2, ATT_HEAD_DIM)
    k = k.reshape(bsz, seq, ATT_HEADS, 2, ATT_HEAD_DIM)
    v = v.reshape(bsz, seq, ATT_HEADS, 2 * ATT_HEAD_DIM)
    lam_init = 0.8 - 0.6 * math.exp(-0.3 * layer_idx)
    lq = lam_q.astype(f32)
    lk = lam_k.astype(f32)
    lam = jnp.exp(jnp.sum(lq[0] * lk[0])) - jnp.exp(jnp.sum(lq[1] * lk[1])) + lam_init
    if cache_k is None:
        nb = seq // Q_BLOCK
        qb = jnp.moveaxis(q.reshape(bsz, nb, Q_BLOCK, ATT_HEADS, 2, ATT_HEAD_DIM), 1, 0)
        starts = jnp.arange(nb, dtype=jnp.int32) * Q_BLOCK
        k_pos = jnp.arange(seq, dtype=jnp.int32)

        def one_block(args):
            q_blk, st = args
            q_pos = st + jnp.arange(Q_BLOCK, dtype=jnp.int32)
            return diff_attend(q_blk, k, v, q_pos, k_pos, lam)

        o = lax.map(one_block, (qb, starts))
        o = jnp.moveaxis(o, 0, 1).reshape(bsz, seq, ATT_HEADS, 2 * ATT_HEAD_DIM)
    else:
        past = cache_k.shape[1]
        k_all = jnp.concatenate(
            [cache_k.astype(k.dtype).reshape(bsz, past, ATT_HEADS, 2, ATT_HEAD_DIM), k], axis=1)
        v_all = jnp.concatenate([cache_v.astype(v.dtype), v], axis=1)
        q_pos = past + jnp.arange(seq, dtype=jnp.int32)
        k_pos = jnp.arange(past + seq, dtype=jnp.int32)
        o = diff_attend(q, k_all, v_all, q_pos, k_pos, lam)
    o = rmsnorm(o, subln_w) * (1.0 - lam_init)
    out = o.reshape(bsz, seq, ATT_V_DIM).astype(x.dtype) @ w_o
    return out, k.reshape(bsz, seq, ATT_HEADS, 2 * ATT_HEAD_DIM), v


def setup_inputs(seed: int = 0) -> dict:
    key = jax.random.key(seed)
    ks = jax.random.split(key, 24)
    f32 = jnp.float32

    def nrm(k, shape, scale):
        return jax.random.normal(k, shape, f32) * scale

    def gain(k, shape):
        return 1.0 + 0.02 * jax.random.normal(k, shape, f32)

    dt0 = jnp.exp(jax.random.uniform(ks[11], (N_SSD_LAYERS, SSD_HEADS), f32,
                                     math.log(1e-3), math.log(1e-1)))
    dt_bias = dt0 + jnp.log(-jnp.expm1(-dt0))
    a_log = jnp.log(jax.random.uniform(ks[12], (N_SSD_LAYERS, SSD_HEADS), f32, 1.0, 16.0))
    return {
        'x_prompt': nrm(ks[0], (BATCH, SEQ, D_MODEL), 1.0),
        'x_sample': nrm(ks[1], (DEC_BATCH, DEC_SEQ, D_MODEL), 1.0),
        'cache_k': nrm(ks[2], (N_ATT_LAYERS, DEC_BATCH, PAST_LEN, ATT_HEADS, 2 * ATT_HEAD_DIM), 1.0),
        'cache_v': nrm(ks[3], (N_ATT_LAYERS, DEC_BATCH, PAST_LEN, ATT_HEADS, 2 * ATT_HEAD_DIM), 1.0),
        'state_ssm': nrm(ks[4], (N_SSD_LAYERS, DEC_BATCH, SSD_HEADS, SSD_HEAD_DIM, D_STATE), 0.1),
        'state_conv': nrm(ks[5], (N_SSD_LAYERS, DEC_BATCH, D_CONV - 1, CONV_DIM), 1.0),
        'norm_mix_w': gain(ks[6], (DEPTH, D_MODEL)),
        'norm_mlp_w': gain(ks[7], (DEPTH, D_MODEL)),
        'final_norm_w': gain(ks[8], (D_MODEL,)),
        'ssd_w_in': nrm(ks[9], (N_SSD_LAYERS, D_MODEL, SSD_IN_DIM), D_MODEL ** -0.5),
        'ssd_conv_w': nrm(ks[10], (N_SSD_LAYERS, D_CONV, CONV_DIM), D_CONV ** -0.5),
        'ssd_conv_b': nrm(ks[13], (N_SSD_LAYERS, CONV_DIM), 0.02),
        'ssd_dt_bias': dt_bias,
        'ssd_a_log': a_log,
        'ssd_d': gain(ks[14], (N_SSD_LAYERS, SSD_HEADS)),
        'ssd_norm_w': gain(ks[15], (N_SSD_LAYERS, D_INNER)),
        'ssd_w_out': nrm(ks[16], (N_SSD_LAYERS, D_INNER, D_MODEL), D_INNER ** -0.5),
        'att_w_qkv': nrm(ks[17], (N_ATT_LAYERS, D_MODEL, 2 * ATT_QK_DIM + ATT_V_DIM), D_MODEL ** -0.5),
        'att_lam_q': nrm(ks[18], (N_ATT_LAYERS, 2, ATT_HEAD_DIM), 0.1),
        'att_lam_k': nrm(ks[19], (N_ATT_LAYERS, 2, ATT_HEAD_DIM), 0.1),
        'att_subln_w': gain(ks[20], (N_ATT_LAYERS, 2 * ATT_HEAD_DIM)),
        'att_w_o': nrm(ks[21], (N_ATT_LAYERS, ATT_V_DIM, D_MODEL), ATT_V_DIM ** -0.5),
        'mlp_w_up': nrm(ks[22], (DEPTH, D_MODEL, D_FF), D_MODEL ** -0.5),
        'mlp_w_down': nrm(ks[23], (DEPTH, D_FF, D_MODEL), D_FF ** -0.5),
    }


def reference(x_prompt, x_sample, cache_k, cache_v, state_ssm, state_conv,
              norm_mix_w, norm_mlp_w, final_norm_w,
              ssd_w_in, ssd_conv_w, ssd_conv_b, ssd_dt_bias, ssd_a_log, ssd_d, ssd_norm_w, ssd_w_out,
              att_w_qkv, att_lam_q, att_lam_k, att_subln_w, att_w_o,
              mlp_w_up, mlp_w_down):
    hp, hs = x_prompt, x_sample
    bp = x_prompt.shape[0]
    k_new_p, v_new_p, ssm_new_p, conv_new_p = [], [], [], []
    k_new_s, v_new_s, ssm_new_s, conv_new_s = [], [], [], []
    for i in range(DEPTH):
        j = i // N_MIXERS
        xp_n = rmsnorm(hp, norm_mix_w[i])
        xs_n = rmsnorm(hs, norm_mix_w[i])
        if i % N_MIXERS == 0:
            params = (ssd_w_in[j], ssd_conv_w[j], ssd_conv_b[j], ssd_dt_bias[j], ssd_a_log[j],
                      ssd_d[j], ssd_norm_w[j], ssd_w_out[j])
            zero_conv = jnp.zeros((bp, D_CONV - 1, CONV_DIM), hp.dtype)
            zero_ssm = jnp.zeros((bp, SSD_HEADS, SSD_HEAD_DIM, D_STATE), jnp.float32)
            op, cp, sp = ssd_mixer(xp_n, zero_conv, zero_ssm, *params)
            os_, cs, ss = ssd_mixer(xs_n, state_conv[j], state_ssm[j], *params)
            conv_new_p.append(cp)
            ssm_new_p.append(sp)
            conv_new_s.append(cs)
            ssm_new_s.append(ss)
        else:
            params = (att_w_qkv[j], att_lam_q[j], att_lam_k[j], att_subln_w[j], att_w_o[j])
            op, kp, vp = diff_attn_mixer(xp_n, None, None, *params, i)
            os_, ks_, vs_ = diff_attn_mixer(xs_n, cache_k[j], cache_v[j], *params, i)
            k_new_p.append(kp)
            v_new_p.append(vp)
            k_new_s.append(ks_)
            v_new_s.append(vs_)
        hp = hp + op
        hs = hs + os_
        hp = hp + sqrelu_mlp(rmsnorm(hp, norm_mlp_w[i]), mlp_w_up[i], mlp_w_down[i])
        hs = hs + sqrelu_mlp(rmsnorm(hs, norm_mlp_w[i]), mlp_w_up[i], mlp_w_down[i])
    y_prompt = rmsnorm(hp, final_norm_w)
    y_sample = rmsnorm(hs, final_norm_w)
    return (y_prompt, y_sample,
            jnp.stack(k_new_p), jnp.stack(v_new_p), jnp.stack(ssm_new_p), jnp.stack(conv_new_p),
            jnp.stack(k_new_s), jnp.stack(v_new_s), jnp.stack(ssm_new_s), jnp.stack(conv_new_s))
```

```python
import math
import os
from contextlib import ExitStack

import numpy as np
import ml_dtypes
import concourse.bass as bass
import concourse.mybir as mybir
from concourse.bass_utils import run_bass_kernel_spmd

F32 = mybir.dt.float32
BF16 = mybir.dt.bfloat16
AF = mybir.ActivationFunctionType
ALU = mybir.AluOpType
AX = mybir.AxisListType

D = 2048
DFF = 8192
DIN = 4096
NH = 64
HP = 64
NG = 8
DST = 128
CONVD = 6144
SSD_IN = 10304
AH = 8
DS = 16
EPS = 1e-5
NRING = 8
NWB = 3


class Cfg:
    def __init__(self, SEQ=4096, PAST=2048, passes=None, depth=4):
        self.SEQ = SEQ
        self.PAST = PAST
        self.depth = depth
        self.passes = passes


class Buf:
    __slots__ = ("w", "r", "name")

    def __init__(self, name=""):
        self.w = None
        self.r = {}
        self.name = name


class Tile:
    def __init__(self, t, name=""):
        self.t = t
        self.b = Buf(name)

    def __getitem__(self, k):
        return self.t[k]


class PEProxy:
    def __init__(self, pe):
        self.pe = pe
        self.mode = None

    @staticmethod
    def _r(v):
        return 32 if v <= 32 else (64 if v <= 64 else 128)

    def _switch(self, lhsT):
        sh = lhsT.shape
        m = (self._r(sh[0]), self._r(int(np.prod(sh[1:]))))
        if m != self.mode:
            if self.mode is not None:
                self.pe.drain()
            self.mode = m

    def matmul(self, out, lhsT, rhs, **kw):
        self._switch(lhsT)
        return self.pe.matmul(out, lhsT, rhs, **kw)

    def transpose(self, out, in_, identity):
        self._switch(in_)
        return self.pe.transpose(out, in_, identity)

    def wait_ge(self, sem, val):
        return self.pe.wait_ge(sem, val)


class Sync:
    def __init__(self, nc, stack):
        self.nc = nc
        self.eng = {"pe": PEProxy(nc.tensor), "act": nc.scalar, "dve": nc.vector, "pool": nc.gpsimd, "sp": nc.sync}
        self.semobj = {}
        self.cnt = {}
        self.seen = {}
        for k in self.eng:
            self.semobj[("e", k)] = stack.enter_context(nc.semaphore("s_" + k))
            self.cnt[k] = 0
            self.seen[k] = {}
        self.ring_val = {}
        self.ring_pos = {}
        for q in ("sp", "pool"):
            self.ring_val[q] = [0] * NRING
            self.ring_pos[q] = 0
            for i in range(NRING):
                self.semobj[("d", q, i)] = stack.enter_context(nc.semaphore(f"d_{q}{i}"))
        sw = not os.environ.get("NOSELF")
        self.same_engine_wait = {"pe": False, "act": sw, "dve": sw, "pool": sw, "sp": False}
        self.nops = 0

    def _wait(self, e, tok):
        key, val = tok
        if self.seen[e].get(key, 0) >= val:
            return
        if key == ("e", e) and not self.same_engine_wait[e]:
            return
        self.eng[e].wait_ge(self.semobj[key], val)
        self.seen[e][key] = val

    def _deps(self, e, reads, writes):
        for b in reads:
            if b.w is not None:
                self._wait(e, b.w)
        for b in writes:
            if b.w is not None:
                self._wait(e, b.w)
            for k, v in b.r.items():
                self._wait(e, (k, v))

    @staticmethod
    def _bufs(xs):
        return [x.b if isinstance(x, Tile) else x for x in xs]

    def _commit(self, tok, reads, writes):
        for b in reads:
            b.r[tok[0]] = tok[1]
        for b in writes:
            b.w = tok
            b.r = {}

    def op(self, e, fn, reads=(), writes=()):
        reads = self._bufs(reads)
        writes = self._bufs(writes)
        self._deps(e, reads, writes)
        ins = fn(self.eng[e])
        self.cnt[e] += 1
        ins.then_inc(self.semobj[("e", e)], 1)
        self._commit((("e", e), self.cnt[e]), reads, writes)
        self.nops += 1
        return ins

    def dma(self, q, out, in_, reads=(), writes=(), **kw):
        reads = self._bufs(reads)
        writes = self._bufs(writes)
        self._deps(q, reads, writes)
        i = self.ring_pos[q]
        self.ring_pos[q] = (i + 1) % NRING
        key = ("d", q, i)
        prev = self.ring_val[q][i]
        if prev > 0:
            self._wait(q, (key, prev))
        val = prev + 16
        self.ring_val[q][i] = val
        self.eng[q].dma_start(out=out, in_=in_, **kw).then_inc(self.semobj[key], 16)
        self._commit((key, val), reads, writes)
        self.nops += 1

    def all_tokens(self):
        toks = []
        for k in self.eng:
            if self.cnt[k] > 0:
                toks.append((("e", k), self.cnt[k]))
        for q in self.ring_val:
            for i, v in enumerate(self.ring_val[q]):
                if v > 0:
                    toks.append((("d", q, i), v))
        return toks

    def barrier(self, engines=None):
        toks = self.all_tokens()
        for e in (engines or list(self.eng)):
            for t in toks:
                key, val = t
                if self.seen[e].get(key, 0) >= val:
                    continue
                if key == ("e", e):
                    if e in ("sp",):
                        continue
                self.eng[e].wait_ge(self.semobj[key], val)
                self.seen[e][key] = val


class Blk:
    def __init__(self, kind, r0, nt, rows, si=0):
        self.kind = kind
        self.r0 = r0
        self.nt = nt
        self.rows = rows
        self.si = si
        self.T = nt * rows


class Prog:
    def __init__(self, cfg):
        self.cfg = cfg
        self.nc = bass.Bass("TRN2", target_bir_lowering=False)
        self.stack = ExitStack()
        self.S = Sync(self.nc, self.stack)
        self.din = {}
        self.dout = {}
        self.declare()

    def inp(self, name, shape, dt=F32):
        self.din[name] = self.nc.dram_tensor(name, list(shape), dt, kind="ExternalInput").ap()
        return self.din[name]

    def outp(self, name, shape, dt=F32):
        self.dout[name] = self.nc.dram_tensor(name, list(shape), dt, kind="ExternalOutput").ap()
        return self.dout[name]

    def scr(self, name, shape, dt):
        return self.nc.dram_tensor(name, list(shape), dt).ap()

    def declare(self):
        c = self.cfg
        SEQ, PAST = c.SEQ, c.PAST
        i = self.inp
        i("xp", [SEQ, D]); i("xs", [2, DS, D])
        i("ck", [2, 2, PAST, 2048]); i("cv", [2, 2, PAST, 2048])
        i("sssm", [2, 2, NH * HP, DST]); i("sconv", [2, 2, 3, CONVD])
        i("nmix", [4, D]); i("nmlp", [4, D]); i("fnw", [1, D])
        i("w_in", [2, D, SSD_IN]); i("convw", [2, 4, CONVD]); i("convb", [2, 1, CONVD])
        i("dtb", [2, NH]); i("alog", [2, NH]); i("dsk", [2, NH]); i("snw", [2, DIN])
        i("w_out", [2, DIN, D]); i("wqkv", [2, D, 6144]); i("lamq", [2, 256]); i("lamk", [2, 256])
        i("subw", [2, 256]); i("wo", [2, D, D]); i("wup", [4, D, DFF]); i("wdn", [4, DFF, D])
        i("ident_bf", [128, 128], BF16); i("ident_f", [128, 128]); i("tri_f", [128, 128]); i("ones_f", [128, 128])
        i("onehot_bf", [128, 64, 128], BF16)
        self.NA = SEQ + 1024
        i("augl", [128, AH, 128], BF16); i("augr", [128, self.NA], BF16); i("diagb", [128, AH, 128], BF16)
        o = self.outp
        o("y_p", [SEQ, D]); o("y_s", [2, DS, D])
        o("k_p", [2, SEQ, 2048]); o("v_p", [2, SEQ, 2048])
        o("ssm_p", [2, NH * HP, DST]); o("conv_p", [2, 3, CONVD])
        o("k_s", [2, 2, DS, 2048]); o("v_s", [2, 2, DS, 2048])
        o("ssm_s", [2, 2, NH * HP, DST]); o("conv_s", [2, 2, 3, CONVD])
        self.hP = [self.scr(f"hP{j}", [SEQ, D], F32) for j in range(2)]
        self.hS = [self.scr(f"hS{j}", [2, DS, D], F32) for j in range(2)]
        self.kT_scr = self.scr("kT_scr", [16, 128, SEQ], BF16)
        self.v_scr = self.scr("v_scr", [SEQ, 2048], BF16)
        self.wspec = {
            "ssd_in": ("w_in", 2, D, SSD_IN), "ssd_out": ("w_out", 2, DIN, D),
            "qkv": ("wqkv", 2, D, 6144), "wo": ("wo", 2, D, D),
            "up": ("wup", 4, D, DFF), "down": ("wdn", 4, DFF, D),
        }
        self.pid = {}
        self.wscr = []
        n = 0
        for name, (_, nl, K, N) in self.wspec.items():
            for l in range(nl):
                cnt = (K // 2048) * ((N + 511) // 512)
                ten = self.scr(f"wscr_{name}{l}", [cnt, 128, 16, 512], BF16)
                j = 0
                for kg in range(K // 2048):
                    for nb in range((N + 511) // 512):
                        self.pid[(name, l, kg, nb)] = n
                        self.pcols = getattr(self, "pcols", [])
                        self.pcols.append(min(512, N - nb * 512))
                        self.wscr.append(ten[j])
                        n += 1
                        j += 1
        self.npieces = n
        self.wscr_buf = [Buf(f"wscr{j}") for j in range(n)]
        nt128 = SEQ // 128
        self.hbuf = [[Buf() for _ in range(nt128)] for _ in range(2)]
        self.hsbuf = [[Buf() for _ in range(2)] for _ in range(2)]
        self.kvscr_buf = [Buf() for _ in range(nt128)]
        self.outbuf = Buf("outs")

    def piece_cols(self, name, nb):
        N = self.wspec[name][3]
        return min(512, N - nb * 512)

    def sb(self, stack, name, shape, dt):
        self._uid = getattr(self, "_uid", 0) + 1
        name = f"sb{self._uid}_{name}"
        return Tile(stack.enter_context(self.nc.sbuf_tensor(name, list(shape), dt)), name)

    def psum_next(self, hold=False):
        held = getattr(self, "psum_held", None)
        if held is None:
            held = self.psum_held = set()
        while True:
            i = self.psum_i
            self.psum_i = (self.psum_i + 1) % len(self.psum)
            if i not in held:
                break
        if hold:
            held.add(i)
        return self.psum[i]

    def psum_release(self, banks):
        for b in banks:
            self.psum_held.discard(self.psum.index(b))

    def hsrc(self, idx, blk, t):
        if blk.kind == "p":
            r = blk.r0 + t * 128
            if idx < 0:
                return self.din["xp"][r:r + 128, :], Buf()
            return self.hP[idx][r:r + 128, :], self.hbuf[idx][r // 128]
        if idx < 0:
            return self.din["xs"][blk.si], Buf()
        return self.hS[idx][blk.si], self.hsbuf[idx][blk.si]

    def blocks(self, T):
        bl = [Blk("p", r, T // 128, 128) for r in range(0, self.cfg.SEQ, T)]
        bl += [Blk("s", 0, 1, DS, 0), Blk("s", 0, 1, DS, 1)]
        return bl

    def emit_weight_conversion(self, order):
        S = self.S
        for (name, l) in order:
            dname, nl, K, N = self.wspec[name]
            W = self.din[dname][l]
            for kg in range(K // 2048):
                for nb in range((N + 511) // 512):
                    ncol = self.piece_cols(name, nb)
                    p = self.pid[(name, l, kg, nb)]
                    src = W[kg * 2048:(kg + 1) * 2048, nb * 512:nb * 512 + ncol].rearrange("(c p) n -> p c n", p=128)
                    dst = self.wscr[p][:, :, 0:ncol]
                    self.conv_queue.append((dst, src, p))

    def convert_some(self, n):
        for _ in range(min(n, len(self.conv_queue))):
            dst, src, p = self.conv_queue.pop(0)
            self.S.dma("pool", dst, src, reads=[], writes=[self.wscr_buf[p]])

    def wget(self, p):
        S = self.S
        if os.environ.get("KSTOP"):
            wb = self.wbuf[self.w_i % NWB]
            self.w_i += 1
            nco = self.pcols[p]
            S.dma("sp", wb[:, :, 0:nco], self.wscr[p][:, :, 0:nco], reads=[self.wscr_buf[p]], writes=[wb])
            return wb
        assert self.wsched[self.w_i] == p, (self.w_i, self.wsched[self.w_i], p)
        while self.w_loaded < min(len(self.wsched), self.w_i + NWB):
            j = self.w_loaded
            pj = self.wsched[j]
            wb = self.wbuf[j % NWB]
            nco = self.pcols[pj]
            S.dma("sp", wb[:, :, 0:nco], self.wscr[pj][:, :, 0:nco], reads=[self.wscr_buf[pj]], writes=[wb])
            self.w_loaded += 1
        wb = self.wbuf[self.w_i % NWB]
        self.w_i += 1
        return wb

    def gemm_b(self, blk, lhsT_fn, lhsT_bufs, name, layer, nbs, nkg, epilogue):
        S = self.S
        for nb in nbs:
            ncol = self.piece_cols(name, nb)
            banks = [self.psum_next() for _ in range(blk.nt)]
            for kg in range(nkg):
                wb = self.wget(self.pid[(name, layer, kg, nb)])
                for t in range(blk.nt):
                    for cc in range(16):
                        first = (kg == 0 and cc == 0)
                        last = (kg == nkg - 1 and cc == 15)
                        S.op("pe", lambda e, t=t, cc=cc, kg=kg: e.matmul(
                            banks[t][:blk.rows, 0:ncol], lhsT_fn(t, kg * 16 + cc), wb[:, cc, 0:ncol],
                            start=first, stop=last), reads=[wb] + lhsT_bufs, writes=[banks[t]])
            for t in range(blk.nt):
                epilogue(nb, t, banks[t], ncol)

    def gemm_a(self, blk, xnT, name, layer, nbs, epilogue):
        S = self.S
        T = blk.T
        for nb in nbs:
            ncol = self.piece_cols(name, nb)
            wb = self.wget(self.pid[(name, layer, 0, nb)])
            for m in range(ncol // 128):
                bank = self.psum_next()
                for cc in range(16):
                    S.op("pe", lambda e, cc=cc, m=m: e.matmul(
                        bank[:, 0:T], wb[:, cc, m * 128:(m + 1) * 128], xnT[:, cc, 0:T],
                        start=(cc == 0), stop=(cc == 15)), reads=[wb, xnT], writes=[bank])
                epilogue(nb, m, bank)

    def row_to_cols(self, dst_tile, dst_ap, src_row_ap, C):
        S = self.S
        tmp = self.rowtmp
        S.dma("sp", tmp[:C, :], src_row_ap.rearrange("(c p) -> c p", p=128), reads=[], writes=[tmp])
        bank = self.psum_next()
        S.op("pe", lambda e: e.transpose(bank[:, 0:C], tmp[:C, :], self.ident_f[:C, :C]), reads=[tmp, self.ident_f], writes=[bank])
        S.op("dve", lambda e: e.tensor_copy(dst_ap, bank[:, 0:C]), reads=[bank], writes=[dst_tile])

    def bcast_row(self, dst_tile, src_row_ap, n):
        self.S.dma("sp", dst_tile[:, 0:n], src_row_ap.partition_broadcast(128), reads=[], writes=[dst_tile])

    def prep_xnT(self, blk, src_idx, wcol, xnT, hin, xs_all, stat):
        S = self.S
        rows = blk.rows
        for t in range(blk.nt):
            ap, hb = self.hsrc(src_idx, blk, t)
            S.dma("sp", hin[:rows, :], ap, reads=[hb], writes=[hin])
            S.op("act", lambda e, t=t: e.activation(xs_all[:rows, t, :], hin[:rows, :], AF.Square), reads=[hin], writes=[xs_all])
            S.op("dve", lambda e, t=t: e.reduce_sum(stat[:rows, t:t + 1], xs_all[:rows, t, :], axis=AX.X), reads=[xs_all], writes=[stat])
            S.op("dve", lambda e, t=t: e.tensor_scalar(stat[:rows, t:t + 1], stat[:rows, t:t + 1], 1.0 / D, EPS, ALU.mult, ALU.add), reads=[stat], writes=[stat])
            S.op("act", lambda e, t=t: e.activation(stat[:rows, t:t + 1], stat[:rows, t:t + 1], AF.Sqrt), reads=[stat], writes=[stat])
            S.op("dve", lambda e, t=t: e.reciprocal(stat[:rows, t:t + 1], stat[:rows, t:t + 1]), reads=[stat], writes=[stat])
            S.op("act", lambda e, t=t: e.activation(xs_all[:rows, t, :], hin[:rows, :], AF.Copy, scale=stat[:rows, t:t + 1]), reads=[hin, stat], writes=[xs_all])
        for cc in range(16):
            bank = self.psum_next()
            bv = bank[:].bitcast(BF16)
            for t in range(blk.nt):
                S.op("pe", lambda e, t=t, cc=cc: e.transpose(bv[:, t * rows:(t + 1) * rows], xs_all[:rows, t, cc * 128:(cc + 1) * 128], self.ident_bf[:rows, :rows]),
                     reads=[xs_all, self.ident_bf], writes=[bank])
            if cc % 2 == 0:
                S.op("act", lambda e, cc=cc: e.activation(xnT[:, cc, 0:blk.T], bv[:, 0:blk.T], AF.Copy, scale=wcol[:, cc:cc + 1]), reads=[bank, wcol], writes=[xnT])
            else:
                S.op("dve", lambda e, cc=cc: e.tensor_scalar(xnT[:, cc, 0:blk.T], bv[:, 0:blk.T], wcol[:, cc:cc + 1], None, ALU.mult), reads=[bank, wcol], writes=[xnT])

    def make_residual_epilogue(self, blk, src_idx, dst_idx, rin, rout):
        S = self.S
        state = {"i": 0}

        def ep(nb, t, bank, ncol):
            i = state["i"]
            state["i"] += 1
            ri = rin[i % len(rin)]
            ro = rout[i % len(rout)]
            rows = blk.rows
            sap, sbuf_ = self.hsrc(src_idx, blk, t)
            dap, dbuf = self.hsrc(dst_idx, blk, t)
            S.dma("sp", ri[:rows, 0:ncol], sap[:, nb * 512:nb * 512 + ncol], reads=[sbuf_], writes=[ri])
            S.op("dve", lambda e: e.tensor_tensor(ro[:rows, 0:ncol], bank[:rows, 0:ncol], ri[:rows, 0:ncol], ALU.add), reads=[bank, ri], writes=[ro])
            S.dma("sp", dap[:, nb * 512:nb * 512 + ncol], ro[:rows, 0:ncol], reads=[ro], writes=[dbuf])
        return ep

    def mlp_pieces(self, layer):
        pl = [self.pid[("up", layer, 0, nb)] for nb in range(16)]
        for nb in range(4):
            pl += [self.pid[("down", layer, kg, nb)] for kg in range(4)]
        return pl

    def mlp_pass(self, layer, src_idx, dst_idx):
        S = self.S
        with ExitStack() as st:
            xnT = self.sb(st, "xnT", [128, 16, 512], BF16)
            hin = self.sb(st, "hin", [128, D], F32)
            xs_all = self.sb(st, "xs_all", [128, 4, D], BF16)
            stat = self.sb(st, "stat", [128, 4], F32)
            wcol = self.sb(st, "wcol", [128, 16], F32)
            hT = self.sb(st, "hT", [128, 64, 512], BF16)
            rl = [self.sb(st, f"rl{j}", [128, 512], F32) for j in range(2)]
            rin = [self.sb(st, f"rin{j}", [128, 512], F32) for j in range(3)]
            rout = [self.sb(st, f"rout{j}", [128, 512], F32) for j in range(3)]
            self.row_to_cols(wcol, wcol[:, 0:16], self.din["nmlp"][layer], 16)
            for blk in self.blocks(512):
                T = blk.T
                self.convert_some(8)
                self.prep_xnT(blk, src_idx, wcol, xnT, hin, xs_all, stat)
                cnt = {"i": 0}

                def up_ep(nb, m, bank):
                    r = rl[cnt["i"] % 2]
                    cnt["i"] += 1
                    S.op("act", lambda e: e.activation(r[:, 0:T], bank[:, 0:T], AF.Relu), reads=[bank], writes=[r])
                    S.op("dve", lambda e: e.tensor_tensor(hT[:, nb * 4 + m, 0:T], r[:, 0:T], r[:, 0:T], ALU.mult), reads=[r], writes=[hT])
                self.gemm_a(blk, xnT, "up", layer, range(16), up_ep)
                ep = self.make_residual_epilogue(blk, src_idx, dst_idx, rin, rout)
                self.gemm_b(blk, lambda t, c: hT[:, c, t * blk.rows:(t + 1) * blk.rows], [hT], "down", layer, range(4), 4, ep)
            S.barrier()

    def final_pass(self, src_idx):
        S = self.S
        with ExitStack() as st:
            hin = [self.sb(st, f"fhin{j}", [128, D], F32) for j in range(2)]
            sq = self.sb(st, "fsq", [128, D], F32)
            ho = [self.sb(st, f"fho{j}", [128, D], F32) for j in range(2)]
            stat = self.sb(st, "fstat", [128, 2], F32)
            wrep = self.sb(st, "fwrep", [128, D], F32)
            self.bcast_row(wrep, self.din["fnw"][0], D)
            i = 0
            for blk in self.blocks(512):
                rows = blk.rows
                for t in range(blk.nt):
                    hi = hin[i % 2]; o = ho[i % 2]; j = i % 2
                    i += 1
                    ap, hb = self.hsrc(src_idx, blk, t)
                    S.dma("sp", hi[:rows, :], ap, reads=[hb], writes=[hi])
                    S.op("act", lambda e: e.activation(sq[:rows, :], hi[:rows, :], AF.Square), reads=[hi], writes=[sq])
                    S.op("dve", lambda e: e.reduce_sum(stat[:rows, j:j + 1], sq[:rows, :], axis=AX.X), reads=[sq], writes=[stat])
                    S.op("dve", lambda e: e.tensor_scalar(stat[:rows, j:j + 1], stat[:rows, j:j + 1], 1.0 / D, EPS, ALU.mult, ALU.add), reads=[stat], writes=[stat])
                    S.op("act", lambda e: e.activation(stat[:rows, j:j + 1], stat[:rows, j:j + 1], AF.Sqrt), reads=[stat], writes=[stat])
                    S.op("dve", lambda e: e.reciprocal(stat[:rows, j:j + 1], stat[:rows, j:j + 1]), reads=[stat], writes=[stat])
                    S.op("dve", lambda e: e.scalar_tensor_tensor(o[:rows, :], hi[:rows, :], stat[:rows, j:j + 1], wrep[:rows, :], ALU.mult, ALU.mult), reads=[hi, stat, wrep], writes=[o])
                    if blk.kind == "p":
                        r = blk.r0 + t * 128
                        dst = self.dout["y_p"][r:r + 128, :]
                    else:
                        dst = self.dout["y_s"][blk.si]
                    S.dma("sp", dst, o[:rows, :], reads=[o], writes=[self.outbuf])
            S.barrier()

    def pass_list(self):
        if self.cfg.passes is not None:
            return self.cfg.passes
        pl = []
        for i in range(self.cfg.depth):
            pl.append(("ssd" if i % 2 == 0 else "att", i))
            pl.append(("mlp", i))
        pl.append(("final", 0))
        return pl

    def pass_pieces(self, kind, layer, blk=None):
        if kind == "mlp":
            return self.mlp_pieces(layer)
        if kind == "ssd":
            return self.ssd_pieces(layer // 2)
        if kind == "att":
            return self.att_pieces(layer // 2)
        return []

    def pass_T(self, kind):
        return 256 if kind == "ssd" else 512

    def build(self):
        S = self.S
        st = self.stack
        nc = self.nc
        passes = self.pass_list()
        self.psum = [Tile(st.enter_context(nc.psum_tensor(f"ps{j}", [128, 512], F32)), f"ps{j}") for j in range(8)]
        self.psum_i = 0
        self.wbuf = [self.sb(st, f"wb{j}", [128, 16, 512], BF16) for j in range(NWB)]
        self.ident_bf = self.sb(st, "ident_bf", [128, 128], BF16)
        self.ident_f = self.sb(st, "ident_f", [128, 128], F32)
        self.rowtmp = self.sb(st, "rowtmp", [64, 128], F32)
        S.dma("sp", self.ident_bf[:], self.din["ident_bf"], writes=[self.ident_bf])
        S.dma("sp", self.ident_f[:], self.din["ident_f"], writes=[self.ident_f])
        self.wsched = []
        conv_order = []
        for kind, layer in passes:
            if kind == "final":
                continue
            per_blk = self.pass_pieces(kind, layer)
            nblk = len(self.blocks(self.pass_T(kind)))
            self.wsched += per_blk * nblk
            if kind == "mlp":
                conv_order += [("up", layer), ("down", layer)]
            elif kind == "ssd":
                conv_order += [("ssd_in", layer // 2), ("ssd_out", layer // 2)]
            elif kind == "att":
                conv_order += [("qkv", layer // 2), ("wo", layer // 2)]
        self.w_i = 0
        self.w_loaded = 0
        self.conv_queue = []
        self.emit_weight_conversion(conv_order)
        nfirst = 0
        for kind, layer in passes[:2]:
            if kind != "final":
                nfirst += len(self.pass_pieces(kind, layer))
        self.convert_some(nfirst)
        src = -1
        nxt = 0
        for kind, layer in passes:
            if kind == "final":
                self.final_pass(src)
                continue
            if kind == "mlp":
                self.mlp_pass(layer, src, nxt)
            elif kind == "ssd":
                self.ssd_pass(layer // 2, layer, src, nxt)
            elif kind == "att":
                self.att_pass(layer // 2, layer, src, nxt)
            src = nxt
            nxt = 1 - nxt
        S.barrier()
        self.stack.close()
        return nc


def host_tables(SEQ):
    bf = ml_dtypes.bfloat16
    t = {}
    t["ident_bf"] = np.eye(128, dtype=np.float32).astype(bf)
    t["ident_f"] = np.eye(128, dtype=np.float32)
    t["tri_f"] = np.triu(np.ones((128, 128), dtype=np.float32))
    t["ones_f"] = np.ones((128, 128), dtype=np.float32)
    oh = np.zeros((128, 64, 128), dtype=np.float32)
    for h in range(64):
        oh[h, h, :] = 1.0
    t["onehot_bf"] = oh.astype(bf)
    slopes = 2.0 ** (-8.0 * np.arange(1, AH + 1) / AH)
    kl = np.arange(128, dtype=np.float32)
    augl = np.zeros((128, AH, 128), dtype=np.float32)
    for h in range(AH):
        augl[0, h] = slopes[h] * kl
        augl[1, h] = slopes[h]
        augl[2, h] = slopes[h]
    t["augl"] = augl.astype(bf)
    NA = SEQ + 1024
    ii = np.arange(NA)
    augr = np.zeros((128, NA), dtype=np.float32)
    augr[0] = 1.0
    augr[1] = -128.0 * (ii // 128)
    augr[2] = -(ii % 128).astype(np.float32)
    t["augr"] = augr.astype(bf)
    k = np.arange(128)[:, None]
    q = np.arange(128)[None, :]
    vis = (k // 64) <= (q // 64)
    diag = np.zeros((128, AH, 128), dtype=np.float32)
    for h in range(AH):
        diag[:, h, :] = np.where(vis, -slopes[h] * np.abs(q - k), -30000.0)
    t["diagb"] = diag.astype(bf)
    return t


def make_in_maps(inputs, cfg, ncores):
    SEQ, PAST = cfg.SEQ, cfg.PAST
    tabs = host_tables(SEQ)
    f = lambda a: np.ascontiguousarray(np.asarray(a, dtype=np.float32))
    shared = {
        "nmix": f(inputs["norm_mix_w"]), "nmlp": f(inputs["norm_mlp_w"]), "fnw": f(inputs["final_norm_w"]).reshape(1, D),
        "w_in": f(inputs["ssd_w_in"]), "convw": f(inputs["ssd_conv_w"]), "convb": f(inputs["ssd_conv_b"]).reshape(2, 1, CONVD),
        "dtb": f(inputs["ssd_dt_bias"]), "alog": f(inputs["ssd_a_log"]), "dsk": f(inputs["ssd_d"]), "snw": f(inputs["ssd_norm_w"]),
        "w_out": f(inputs["ssd_w_out"]), "wqkv": f(inputs["att_w_qkv"]), "lamq": f(inputs["att_lam_q"]).reshape(2, 256),
        "lamk": f(inputs["att_lam_k"]).reshape(2, 256), "subw": f(inputs["att_subln_w"]), "wo": f(inputs["att_w_o"]),
        "wup": f(inputs["mlp_w_up"]), "wdn": f(inputs["mlp_w_down"]),
    }
    shared.update(tabs)
    maps = []
    xp = f(inputs["x_prompt"]); xs = f(inputs["x_sample"])
    ck = f(inputs["cache_k"]); cv = f(inputs["cache_v"])
    ssm = f(inputs["state_ssm"]); cvs = f(inputs["state_conv"])
    for c in range(ncores):
        m = dict(shared)
        m["xp"] = np.ascontiguousarray(xp[c, :SEQ])
        m["xs"] = np.ascontiguousarray(xs[2 * c:2 * c + 2])
        m["ck"] = np.ascontiguousarray(ck[:, 2 * c:2 * c + 2, :PAST].reshape(2, 2, PAST, 2048))
        m["cv"] = np.ascontiguousarray(cv[:, 2 * c:2 * c + 2, :PAST].reshape(2, 2, PAST, 2048))
        m["sssm"] = np.ascontiguousarray(ssm[:, 2 * c:2 * c + 2].reshape(2, 2, NH * HP, DST))
        m["sconv"] = np.ascontiguousarray(cvs[:, 2 * c:2 * c + 2])
        maps.append(m)
    return maps


_PROG_CACHE = {}


def run(inputs, cfg, ncores=8, trace=False):
    key = (cfg.SEQ, cfg.PAST, cfg.depth, str(cfg.passes))
    prog = Prog(cfg)
    _add_mixers(prog)
    nc = prog.build()
    maps = make_in_maps(inputs, cfg, ncores)
    res = run_bass_kernel_spmd(nc, maps, core_ids=list(range(ncores)), trace=trace)
    return res, prog


def kernel(**inputs):
    cfg = Cfg()
    res, prog = run(inputs, cfg, 8)
    r = res.results
    SEQ = cfg.SEQ
    y_p = np.stack([r[c]["y_p"] for c in range(8)])
    y_s = np.concatenate([r[c]["y_s"] for c in range(8)], axis=0)
    k_p = np.stack([r[c]["k_p"] for c in range(8)], axis=1).reshape(2, 8, SEQ, AH, 256)
    v_p = np.stack([r[c]["v_p"] for c in range(8)], axis=1).reshape(2, 8, SEQ, AH, 256)
    ssm_p = np.stack([r[c]["ssm_p"] for c in range(8)], axis=1).reshape(2, 8, NH, HP, DST)
    conv_p = np.stack([r[c]["conv_p"] for c in range(8)], axis=1)
    k_s = np.concatenate([r[c]["k_s"] for c in range(8)], axis=1).reshape(2, 16, DS, AH, 256)
    v_s = np.concatenate([r[c]["v_s"] for c in range(8)], axis=1).reshape(2, 16, DS, AH, 256)
    ssm_s = np.concatenate([r[c]["ssm_s"] for c in range(8)], axis=1).reshape(2, 16, NH, HP, DST)
    conv_s = np.concatenate([r[c]["conv_s"] for c in range(8)], axis=1)
    outs = (y_p, y_s, k_p, v_p, ssm_p, conv_p, k_s, v_s, ssm_s, conv_s)
    return tuple(np.ascontiguousarray(o, dtype=np.float32) for o in outs)


def ssd_pieces(self, j):
    pl = [self.pid[("ssd_in", j, 0, nb)] for nb in (16, 17, 18, 19, 20)]
    for g in range(8):
        pl += [self.pid[("ssd_in", j, 0, g)], self.pid[("ssd_in", j, 0, 8 + g)]]
    for nb in range(4):
        pl += [self.pid[("ssd_out", j, kg, nb)] for kg in range(2)]
    return pl


def ssd_pass(self, j, layer, src_idx, dst_idx):
    S = self.S
    din = self.din
    with ExitStack() as st:
        sb = lambda name, shape, dt: self.sb(st, name, shape, dt)
        xnT = sb("xnT", [128, 16, 256], BF16)
        hin = sb("hin", [128, D], F32)
        xs_all = sb("xs_all", [128, 2, D], BF16)
        stat = sb("stat", [128, 4], F32)
        wcol = sb("wcol", [128, 16], F32)
        tri = sb("tri", [128, 128], F32)
        tri_b = sb("tri_b", [128, 128], BF16)
        ones_b = sb("ones_b", [128, 128], BF16)
        av_hi = sb("av_hi", [128, 128], BF16)
        av_lo = sb("av_lo", [128, 128], BF16)
        onehot = sb("onehot", [128, 64, 128], BF16)
        cw = sb("cw", [128, 48, 4], F32)
        cb = sb("cb", [128, 48], F32)
        gnw = sb("gnw", [128, 32], F32)
        dtb_rep = sb("dtb_rep", [128, 64], F32)
        aneg = sb("aneg", [128, 64], F32)
        D_rep = sb("D_rep", [128, 64], F32)
        halo = sb("halo", [128, 48, 3], F32)
        tmpc = sb("tmpc", [128, 48], F32)
        state = sb("state", [128, DIN], F32)
        state_bf = sb("state_bf", [128, DIN], BF16)
        stload = sb("stload", [128, 8, 128], F32)
        BT = sb("BT", [128, 8, 256], BF16)
        CT = sb("CT", [128, 8, 256], BF16)
        B_tok = sb("B_tok", [128, 2, 1024], BF16)
        dtv = sb("dtv", [128, 2, 64], F32)
        lndt = sb("lndt", [128, 2, 64], F32)
        av = sb("av", [128, 2, 64], F32)
        tmp64 = sb("tmp64", [128, 64], F32)
        acum_sb = sb("acum_sb", [128, 2, 64], F32)
        biasS = sb("biasS", [128, 2, 64], F32)
        e_t = sb("e_t", [128, 2, 64], F32)
        tailw = sb("tailw", [128, 2, 64], F32)
        etot = sb("etot", [128, 2, 64], F32)
        acT_f = sb("acT_f", [128, 128], F32)
        acT_hi = sb("acT_hi", [128, 2, 128], BF16)
        acT_lo = sb("acT_lo", [128, 2, 128], BF16)
        cbm = sb("cbm", [128, 16, 128], F32)
        sz_g = sb("sz_g", [128, 2, 512], BF16)
        xT_stage = sb("xT_stage", [128, 4, 256], BF16)
        x_tok_g = sb("x_tok_g", [128, 2, 512], BF16)
        xD = [sb(f"xD{i}", [128, 512], BF16) for i in range(2)]
        stg = [sb(f"stg{i}", [128, 3 + 256], F32) for i in range(2)]
        acc = [sb(f"acc{i}", [128, 256], F32) for i in range(2)]
        dec = [sb(f"dec{i}", [128, 128], F32) for i in range(8)]
        MT = [sb(f"MT{i}", [128, 128], BF16) for i in range(16)]
        t1 = [sb(f"t1_{i}", [128, 512], F32) for i in range(2)]
        t2 = [sb(f"t2_{i}", [128, 512], F32) for i in range(2)]
        gsb = [sb(f"gsb{i}", [128, 512], F32) for i in range(2)]
        gs_bf = [sb(f"gs_bf{i}", [128, 512], BF16) for i in range(2)]
        gss = sb("gss", [128, 2], F32)
        xw = [sb(f"xw{i}", [128, 512], BF16) for i in range(2)]
        gT = sb("gT", [128, 32, 256], BF16)
        rin = [sb(f"rin{i}", [128, 512], F32) for i in range(2)]
        rout = [sb(f"rout{i}", [128, 512], F32) for i in range(2)]

        S.dma("sp", tri[:], din["tri_f"], writes=[tri])
        S.op("dve", lambda e: e.tensor_copy(tri_b[:, :], tri[:, :]), reads=[tri], writes=[tri_b])
        S.op("dve", lambda e: e.memset(ones_b[:, :], 1.0), writes=[ones_b])
        S.dma("sp", onehot[:], din["onehot_bf"], writes=[onehot])
        self.row_to_cols(wcol, wcol[:, 0:16], din["nmix"][layer], 16)
        for jj in range(4):
            self.row_to_cols(cw, cw[:, :, jj], din["convw"][j, jj], 48)
        self.row_to_cols(cb, cb[:, 0:48], din["convb"][j, 0], 48)
        self.row_to_cols(gnw, gnw[:, 0:32], din["snw"][j], 32)
        self.bcast_row(dtb_rep, din["dtb"][j], 64)
        self.bcast_row(D_rep, din["dsk"][j], 64)
        self.bcast_row(aneg, din["alog"][j], 64)
        S.op("act", lambda e: e.activation(aneg[:, :], aneg[:, :], AF.Exp), reads=[aneg], writes=[aneg])
        S.op("dve", lambda e: e.tensor_scalar(aneg[:, :], aneg[:, :], -1.0, None, ALU.mult), reads=[aneg], writes=[aneg])

        ctr = {"conv": 0, "k": 0}
        KSTOP = float(os.environ.get("KSTOP", "99"))
        blocks = self.blocks(256) if KSTOP > 1 else []
        for bi, blk in enumerate(blocks):
            T, rows, nt = blk.T, blk.rows, blk.nt
            CL = rows
            self.convert_some(4)
            first = (bi == 0) or (blocks[bi - 1].kind != blk.kind) or (blocks[bi - 1].si != blk.si)
            last = (bi == len(blocks) - 1) or (blocks[bi + 1].kind != blk.kind) or (blocks[bi + 1].si != blk.si)
            if blk.kind == "p":
                conv_out = self.dout["conv_p"][j]
                ssm_out = self.dout["ssm_p"][j]
            else:
                conv_out = self.dout["conv_s"][j, blk.si]
                ssm_out = self.dout["ssm_s"][j, blk.si]
            if first:
                if blk.kind == "p":
                    S.op("dve", lambda e: e.memset(state[:, :], 0.0), writes=[state])
                    S.op("dve", lambda e: e.memset(state_bf[:, :], 0.0), writes=[state_bf])
                    S.op("dve", lambda e: e.memset(halo[:, :, :], 0.0), writes=[halo])
                else:
                    src = din["sssm"][j, blk.si].rearrange("(q r) n -> r q n", r=128)
                    for qt in range(4):
                        S.dma("sp", stload[:, :, :], src[:, qt * 8:(qt + 1) * 8, :], writes=[stload])
                        for q4 in range(2):
                            bank = self.psum_next()
                            for qq in range(4):
                                q = q4 * 4 + qq
                                S.op("pe", lambda e, q=q, qq=qq: e.transpose(bank[:, qq * 128:(qq + 1) * 128], stload[:, q, :], self.ident_f[:, :]),
                                     reads=[stload, self.ident_f], writes=[bank])
                            c0 = (qt * 8 + q4 * 4) * 128
                            S.op("dve", lambda e, c0=c0: e.tensor_copy(state[:, c0:c0 + 512], bank[:, 0:512]), reads=[bank], writes=[state])
                    S.op("act", lambda e: e.activation(state_bf[:, :], state[:, :], AF.Copy), reads=[state], writes=[state_bf])
                    for jj in range(3):
                        self.row_to_cols(halo, halo[:, :, jj], din["sconv"][j, blk.si, jj], 48)

            self.prep_xnT(blk, src_idx, wcol, xnT, hin, xs_all, stat)
            if KSTOP <= 2:
                continue

            def conv_ep(cc, bank, dest_tile, dest_ap):
                i = ctr["conv"]
                ctr["conv"] += 1
                sg = stg[i % 2]
                ac = acc[i % 2]
                S.op("act", lambda e: e.activation(sg[:, 3:3 + T], bank[:, 0:T], AF.Copy), reads=[bank], writes=[sg])
                S.op("dve", lambda e: e.tensor_copy(sg[:, 0:3], halo[:, cc, :]), reads=[halo], writes=[sg])
                S.op("dve", lambda e: e.tensor_copy(halo[:, cc, :], sg[:, T:T + 3]), reads=[sg], writes=[halo])
                S.op("act", lambda e: e.activation(ac[:, 0:T], sg[:, 0:T], AF.Identity, bias=cb[:, cc:cc + 1], scale=cw[:, cc, 0:1]),
                     reads=[sg, cb, cw], writes=[ac])
                for jj in (1, 2, 3):
                    S.op("dve", lambda e, jj=jj: e.scalar_tensor_tensor(ac[:, 0:T], sg[:, jj:jj + T], cw[:, cc, jj:jj + 1], ac[:, 0:T], ALU.mult, ALU.add),
                         reads=[sg, cw, ac], writes=[ac])
                S.op("act", lambda e: e.activation(dest_ap, ac[:, 0:T], AF.Silu), reads=[ac], writes=[dest_tile])

            def bc_ep(nb, m, bank):
                cc = (nb - 8) * 4 + m
                gg = (cc - 32) % 8
                if cc < 40:
                    conv_ep(cc, bank, BT, BT[:, gg, 0:T])
                else:
                    conv_ep(cc, bank, CT, CT[:, gg, 0:T])
            self.gemm_a(blk, xnT, "ssd_in", j, (16, 17, 18, 19), bc_ep)
            if KSTOP <= 3:
                continue

            def dt_ep(nb, t, bank, ncol):
                S.op("dve", lambda e: e.tensor_tensor(dtv[:rows, t, :], bank[:rows, 0:64], dtb_rep[:rows, :], ALU.add), reads=[bank, dtb_rep], writes=[dtv])
                S.op("act", lambda e: e.activation(tmp64[:rows, :], dtv[:rows, t, :], AF.Exp), reads=[dtv], writes=[tmp64])
                S.op("dve", lambda e: e.tensor_scalar(tmp64[:rows, :], tmp64[:rows, :], 1.0, None, ALU.add), reads=[tmp64], writes=[tmp64])
                S.op("act", lambda e: e.activation(dtv[:rows, t, :], tmp64[:rows, :], AF.Ln), reads=[tmp64], writes=[dtv])
                S.op("act", lambda e: e.activation(lndt[:rows, t, :], dtv[:rows, t, :], AF.Ln), reads=[dtv], writes=[lndt])
                S.op("dve", lambda e: e.tensor_tensor(av[:rows, t, :], dtv[:rows, t, :], aneg[:rows, :], ALU.mult), reads=[dtv, aneg], writes=[av])
            self.gemm_b(blk, lambda t, c: xnT[:, c, t * rows:(t + 1) * rows], [xnT], "ssd_in", j, (20,), 1, dt_ep)

            for t in range(nt):
                bank = self.psum_next()
                bv = bank[:].bitcast(BF16)
                for g in range(8):
                    S.op("pe", lambda e, g=g, t=t: e.transpose(bv[:CL, g * 128:(g + 1) * 128], BT[:, g, t * CL:(t + 1) * CL], self.ident_bf[:, :]),
                         reads=[BT, self.ident_bf], writes=[bank])
                S.op("dve", lambda e, t=t: e.tensor_copy(B_tok[:CL, t, :], bv[:CL, 0:1024]), reads=[bank], writes=[B_tok])

            if KSTOP <= 4:
                continue
            for t in range(nt):
                bankA = self.psum_next()
                S.op("dve", lambda e: e.memset(av_hi[:, :], 0.0), writes=[av_hi])
                S.op("dve", lambda e: e.memset(av_lo[:, :], 0.0), writes=[av_lo])
                S.op("dve", lambda e, t=t: e.tensor_copy(av_hi[:CL, 0:64], av[:CL, t, :]), reads=[av], writes=[av_hi])
                S.op("dve", lambda e, t=t: e.tensor_tensor(av_lo[:CL, 0:64], av[:CL, t, :], av_hi[:CL, 0:64], ALU.subtract), reads=[av, av_hi], writes=[av_lo])
                for ii, avx in enumerate((av_hi, av_lo)):
                    S.op("pe", lambda e, avx=avx, ii=ii: e.matmul(bankA[0:128, 0:CL], avx[:, :], tri_b[:, :CL], start=(ii == 0), stop=(ii == 1)), reads=[avx, tri_b], writes=[bankA])
                for ii, avx in enumerate((av_hi, av_lo)):
                    S.op("pe", lambda e, avx=avx, ii=ii: e.matmul(bankA[0:CL, 128:192], tri_b[:, :CL], avx[:, 0:64], start=(ii == 0), stop=(ii == 1)), reads=[avx, tri_b], writes=[bankA])
                for ii, avx in enumerate((av_hi, av_lo)):
                    S.op("pe", lambda e, avx=avx, ii=ii: e.matmul(bankA[0:128, 256:320], ones_b[:, 0:128], avx[:, 0:64], start=(ii == 0), stop=(ii == 1)), reads=[avx, ones_b], writes=[bankA])
                if KSTOP <= 4.2:
                    continue
                KN = int(os.environ.get('KN', '99'))
                if KN > 0:
                    S.op("dve", lambda e: e.tensor_copy(acT_f[:, 0:CL], bankA[0:128, 0:CL]), reads=[bankA], writes=[acT_f])
                if KN > 1:
                    S.op("dve", lambda e, t=t: e.tensor_copy(acT_hi[:, t, 0:CL], acT_f[:, 0:CL]), reads=[acT_f], writes=[acT_hi])
                if KN > 2:
                    S.op("dve", lambda e, t=t: e.tensor_tensor(acT_lo[:, t, 0:CL], acT_f[:, 0:CL], acT_hi[:, t, 0:CL], ALU.subtract), reads=[acT_f, acT_hi], writes=[acT_lo])
                if KN > 3:
                    S.op("dve", lambda e, t=t: e.tensor_copy(acum_sb[:CL, t, :], bankA[0:CL, 128:192]), reads=[bankA], writes=[acum_sb])
                if KN > 4:
                    S.op("act", lambda e, t=t: e.activation(e_t[:CL, t, :], acum_sb[:CL, t, :], AF.Exp), reads=[acum_sb], writes=[e_t])
                if KN > 5:
                    S.op("dve", lambda e, t=t: e.tensor_tensor(biasS[:CL, t, :], lndt[:CL, t, :], acum_sb[:CL, t, :], ALU.subtract), reads=[lndt, acum_sb], writes=[biasS])
                if KN > 6:
                    S.op("dve", lambda e, t=t: e.tensor_tensor(tmp64[:CL, :], bankA[0:CL, 256:320], biasS[:CL, t, :], ALU.add), reads=[bankA, biasS], writes=[tmp64])
                if KN > 7:
                    S.op("act", lambda e, t=t: e.activation(tailw[:CL, t, :], tmp64[:CL, :], AF.Exp), reads=[tmp64], writes=[tailw])
                if KN > 8:
                    S.op("dve", lambda e, t=t: e.tensor_copy(etot[:, t, :], bankA[0:128, 256:320]), reads=[bankA], writes=[etot]); S.op("act", lambda e, t=t: e.activation(etot[:, t, :], etot[:, t, :], AF.Exp), reads=[etot], writes=[etot])
                for g4 in range(2 if KSTOP > 4.4 else 0):
                    bank = self.psum_next()
                    for gq in range(4):
                        g = g4 * 4 + gq
                        S.op("pe", lambda e, g=g, gq=gq, t=t: e.matmul(bank[0:CL, gq * 128:gq * 128 + CL], BT[:, g, t * CL:(t + 1) * CL], CT[:, g, t * CL:(t + 1) * CL], start=True, stop=True),
                             reads=[BT, CT], writes=[bank])
                    for gq in range(4 if KSTOP > 4.6 else 0):
                        g = g4 * 4 + gq
                        S.op("dve", lambda e, g=g, gq=gq, t=t: e.tensor_tensor(cbm[:CL, t * 8 + g, 0:CL], bank[0:CL, gq * 128:gq * 128 + CL], tri[:CL, :CL], ALU.mult),
                             reads=[bank, tri], writes=[cbm])

            if KSTOP <= 5:
                continue
            for g in range(8):
                def z_ep(nb, t, bank, ncol):
                    S.op("act", lambda e: e.activation(sz_g[:rows, t, :], bank[:rows, 0:512], AF.Silu), reads=[bank], writes=[sz_g])
                self.gemm_b(blk, lambda t, c: xnT[:, c, t * rows:(t + 1) * rows], [xnT], "ssd_in", j, (g,), 1, z_ep)

                def x_ep(nb, m, bank):
                    cc = (nb - 8) * 4 + m
                    conv_ep(cc, bank, xT_stage, xT_stage[:, m, 0:T])
                self.gemm_a(blk, xnT, "ssd_in", j, (8 + g,), x_ep)
                for t in range(nt):
                    bank = self.psum_next()
                    bv = bank[:].bitcast(BF16)
                    for m in range(4):
                        S.op("pe", lambda e, m=m, t=t: e.transpose(bv[:CL, m * 128:(m + 1) * 128], xT_stage[:, m, t * CL:(t + 1) * CL], self.ident_bf[:, :]),
                             reads=[xT_stage, self.ident_bf], writes=[bank])
                    S.op("dve", lambda e, t=t: e.tensor_copy(x_tok_g[:CL, t, :], bv[:CL, 0:512]), reads=[bank], writes=[x_tok_g])

                gc = slice(g * 512, (g + 1) * 512)
                hs = slice(g * 8, (g + 1) * 8)
                for t in range(nt if KSTOP > 6 else 0):
                    k = ctr["k"]
                    ctr["k"] += 1
                    tc = slice(t * CL, (t + 1) * CL)
                    xg = x_tok_g[:CL, t, :]
                    xg3 = xg.rearrange("p (h e) -> p h e", h=8)
                    bankO = self.psum_next()
                    S.op("pe", lambda e: e.matmul(bankO[0:CL, 0:512], CT[:, g, tc], state_bf[:, gc], start=True, stop=True), reads=[CT, state_bf], writes=[bankO])
                    xd = xD[k % 2]
                    S.op("pool", lambda e: e.tensor_tensor(xd[:CL, :].rearrange("p (h e) -> p h e", h=8), xg3, D_rep[:CL, hs].unsqueeze(2).to_broadcast([CL, 8, 64]), ALU.mult),
                         reads=[x_tok_g, D_rep], writes=[xd])
                    bankD = self.psum_next()
                    S.op("pe", lambda e: e.matmul(bankD[0:CL, 0:512], self.ident_bf[:CL, :CL], xd[:CL, :], start=True, stop=False), reads=[xd, self.ident_bf], writes=[bankD])
                    xwk = xw[k % 2]
                    S.op("pool", lambda e: e.tensor_tensor(xwk[:CL, :].rearrange("p (h e) -> p h e", h=8), xg3, tailw[:CL, t, hs].unsqueeze(2).to_broadcast([CL, 8, 64]), ALU.mult),
                         reads=[x_tok_g, tailw], writes=[xwk])
                    Rs = []
                    bankR = None
                    for i in range(8):
                        hh = g * 8 + i
                        if i % 4 == 0:
                            bankR = self.psum_next()
                        Rsl = bankR[0:CL, (i % 4) * 128:(i % 4) * 128 + CL]
                        Rs.append((bankR, Rsl))
                        S.op("pe", lambda e, Rsl=Rsl, hh=hh: e.matmul(Rsl, onehot[:, hh, 0:CL], acT_hi[:, t, 0:CL], start=True, stop=False), reads=[onehot, acT_hi], writes=[bankR])
                        S.op("pe", lambda e, Rsl=Rsl, hh=hh: e.matmul(Rsl, onehot[:, hh, 0:CL], acT_lo[:, t, 0:CL], start=False, stop=True), reads=[onehot, acT_lo], writes=[bankR])
                    bankS = self.psum_next()
                    S.op("pe", lambda e: e.matmul(bankS[0:128, 0:512], B_tok[:CL, t, g * 128:(g + 1) * 128], xwk[:CL, :], start=True, stop=True), reads=[B_tok, xwk], writes=[bankS])
                    for i in range(8):
                        hh = g * 8 + i
                        bankR, Rsl = Rs[i]
                        dc = dec[i]
                        S.op("dve", lambda e, Rsl=Rsl, hh=hh, dc=dc: e.tensor_scalar(dc[:CL, :CL], Rsl, acum_sb[:CL, t, hh:hh + 1], 0.0, ALU.subtract, ALU.min), reads=[bankR, acum_sb], writes=[dc])
                    st3 = state[:, gc].rearrange("p (h e) -> p h e", h=8)
                    S.op("pool", lambda e: e.tensor_tensor(st3, st3, etot[:, t, hs].unsqueeze(2).to_broadcast([128, 8, 64]), ALU.mult), reads=[state, etot], writes=[state])
                    S.op("dve", lambda e: e.tensor_tensor(state[:, gc], state[:, gc], bankS[0:128, 0:512], ALU.add), reads=[state, bankS], writes=[state])
                    S.op("act", lambda e: e.activation(state_bf[:, gc], state[:, gc], AF.Copy), reads=[state], writes=[state_bf])
                    for i in range(8):
                        hh = g * 8 + i
                        dc = dec[i]
                        S.op("act", lambda e, hh=hh, dc=dc: e.activation(dc[:CL, :CL], dc[:CL, :CL], AF.Exp, bias=lndt[:CL, t, hh:hh + 1]), reads=[dc, lndt], writes=[dc])
                    for i in range(8):
                        dc = dec[i]
                        mt = MT[(k % 2) * 8 + i]
                        S.op("pool", lambda e, dc=dc, mt=mt: e.tensor_tensor(mt[:CL, :CL], dc[:CL, :CL], cbm[:CL, t * 8 + g, 0:CL], ALU.mult),
                             reads=[dc, cbm], writes=[mt])
                    for i in range(8):
                        mt = MT[(k % 2) * 8 + i]
                        S.op("pe", lambda e, mt=mt, i=i: e.matmul(bankD[0:CL, i * 64:(i + 1) * 64], mt[:CL, :CL], xg[:, i * 64:(i + 1) * 64], start=False, stop=(i == 7)),
                             reads=[mt, x_tok_g], writes=[bankD])
                    a1 = t1[k % 2]; a2 = t2[k % 2]; gb = gsb[k % 2]; gsf = gs_bf[k % 2]
                    S.op("dve", lambda e: e.tensor_tensor(a1[:CL, :].rearrange("p (h e) -> p h e", h=8), bankO[0:CL, 0:512].rearrange("p (h e) -> p h e", h=8),
                                                          e_t[:CL, t, hs].unsqueeze(2).to_broadcast([CL, 8, 64]), ALU.mult), reads=[bankO, e_t], writes=[a1])
                    S.op("dve", lambda e: e.tensor_tensor(a2[:CL, :], a1[:CL, :], bankD[0:CL, 0:512], ALU.add), reads=[a1, bankD], writes=[a2])
                    S.op("pool", lambda e: e.tensor_tensor(gb[:CL, :], a2[:CL, :], sz_g[:CL, t, :], ALU.mult), reads=[a2, sz_g], writes=[gb])
                    S.op("act", lambda e: e.activation(a1[:CL, :], gb[:CL, :], AF.Square), reads=[gb], writes=[a1])
                    kk = k % 2
                    S.op("dve", lambda e: e.reduce_sum(gss[:CL, kk:kk + 1], a1[:CL, :], axis=AX.X), reads=[a1], writes=[gss])
                    S.op("dve", lambda e: e.tensor_scalar(gss[:CL, kk:kk + 1], gss[:CL, kk:kk + 1], 1.0 / 512, EPS, ALU.mult, ALU.add), reads=[gss], writes=[gss])
                    S.op("act", lambda e: e.activation(gss[:CL, kk:kk + 1], gss[:CL, kk:kk + 1], AF.Sqrt), reads=[gss], writes=[gss])
                    S.op("dve", lambda e: e.reciprocal(gss[:CL, kk:kk + 1], gss[:CL, kk:kk + 1]), reads=[gss], writes=[gss])
                    S.op("act", lambda e: e.activation(gsf[:CL, :], gb[:CL, :], AF.Copy, scale=gss[:CL, kk:kk + 1]), reads=[gb, gss], writes=[gsf])
                    bankT = self.psum_next()
                    bvT = bankT[:].bitcast(BF16)
                    for i in range(4):
                        S.op("pe", lambda e, i=i: e.transpose(bvT[:, i * 128:i * 128 + CL], gsf[:CL, i * 128:(i + 1) * 128], self.ident_bf[:CL, :CL]),
                             reads=[gsf, self.ident_bf], writes=[bankT])
                    for i in range(4):
                        ci = g * 4 + i
                        if False:
                            pass
                        else:
                            S.op("dve", lambda e, i=i, ci=ci: e.tensor_scalar(gT[:, ci, tc], bvT[:, i * 128:i * 128 + CL], gnw[:, ci:ci + 1], None, ALU.mult), reads=[bankT, gnw], writes=[gT])
            ep = self.make_residual_epilogue(blk, src_idx, dst_idx, rin, rout)
            self.gemm_b(blk, lambda t, c: gT[:, c, t * rows:(t + 1) * rows], [gT], "ssd_out", j, range(4), 2, ep)

            if last:
                for jj in range(3):
                    S.op("dve", lambda e, jj=jj: e.tensor_copy(tmpc[:, :], halo[:, :, jj]), reads=[halo], writes=[tmpc])
                    bank = self.psum_next()
                    S.op("pe", lambda e: e.transpose(bank[0:48, 0:128], tmpc[:, 0:48], self.ident_f[:, :]), reads=[tmpc, self.ident_f], writes=[bank])
                    S.op("dve", lambda e: e.tensor_copy(self.rowtmp[0:48, :], bank[0:48, 0:128]), reads=[bank], writes=[self.rowtmp])
                    S.dma("sp", conv_out[jj].rearrange("(c p) -> c p", p=128), self.rowtmp[0:48, :], reads=[self.rowtmp], writes=[self.outbuf])
                dst = ssm_out.rearrange("(q r) n -> r q n", r=128)
                for qt in range(4):
                    for q4 in range(2):
                        bank = self.psum_next()
                        for qq in range(4):
                            q = qt * 8 + q4 * 4 + qq
                            S.op("pe", lambda e, q=q, qq=qq: e.transpose(bank[:, qq * 128:(qq + 1) * 128], state[:, q * 128:(q + 1) * 128], self.ident_f[:, :]),
                                 reads=[state, self.ident_f], writes=[bank])
                        S.op("dve", lambda e, q4=q4: e.tensor_copy(stload[:, q4 * 4:(q4 + 1) * 4, :], bank[:, 0:512].rearrange("p (q n) -> p q n", q=4)), reads=[bank], writes=[stload])
                    S.dma("sp", dst[:, qt * 8:(qt + 1) * 8, :], stload[:, :, :], reads=[stload], writes=[self.outbuf])
        S.barrier()


Prog.ssd_pieces = ssd_pieces
Prog.ssd_pass = ssd_pass

def att_pieces(self, j):
    pl = [self.pid[("qkv", j, 0, nb)] for nb in range(12)]
    pl += [self.pid[("wo", j, 0, nb)] for nb in range(4)]
    return pl


def att_pass(self, j, layer, src_idx, dst_idx):
    S = self.S
    din = self.din
    SEQ, PAST = self.cfg.SEQ, self.cfg.PAST
    NPT = PAST // 128
    lam_init = 0.8 - 0.6 * math.exp(-0.3 * layer)
    with ExitStack() as st:
        sb = lambda name, shape, dt: self.sb(st, name, shape, dt)
        xnT = sb("xnT", [128, 16, 512], BF16)
        hin = sb("hin", [128, D], F32)
        xs_all = sb("xs_all", [128, 4, D], BF16)
        stat = sb("stat", [128, 4], F32)
        wcol = sb("wcol", [128, 16], F32)
        qT = sb("qT", [128, 16, 512], BF16)
        NK = max(SEQ, PAST + 128)
        KT = sb("KT", [128, 2, NK], BF16)
        NKT = NK // 128
        Vext = sb("Vext", [128, NKT, 257], BF16)
        pt = [sb(f"pt{i}", [128, 512], BF16) for i in range(3)]
        o_sb = sb("o_sb", [128, 4, 256], F32)
        osq = sb("osq", [128, 256], F32)
        os_bf = [sb(f"os_bf{i}", [128, 256], BF16) for i in range(2)]
        oT = sb("oT", [128, 16, 512], BF16)
        kst = [sb(f"kst{i}", [128, 512], F32) for i in range(2)]
        kbf = [sb(f"kbf{i}", [128, 512], BF16) for i in range(2)]
        kTst = [sb(f"kTst{i}", [128, 4, 128], BF16) for i in range(2)]
        kTnew = sb("kTnew", [128, 16, DS], BF16)
        vnew = sb("vnew", [DS, 2048], BF16)
        ckst = sb("ckst", [128, max(NPT, 1), 256], BF16)
        diagb = sb("diagb", [128, AH, 128], BF16)
        augl = sb("augl", [128, AH, 128], BF16)
        augr = sb("augr", [128, self.NA], BF16)
        lq = sb("lq", [128, 256], F32)
        lk = sb("lk", [128, 256], F32)
        lsum = sb("lsum", [128, 2], F32)
        neglam = sb("neglam", [128, 1], F32)
        subcol = sb("subcol", [128, 2], F32)
        rsum = sb("rsum", [128, 4], F32)
        rs2 = sb("rs2", [128, 4], F32)
        ss = sb("ss", [128, 4], F32)
        rin = [sb(f"rin{i}", [128, 512], F32) for i in range(2)]
        rout = [sb(f"rout{i}", [128, 512], F32) for i in range(2)]

        S.dma("sp", diagb[:], din["diagb"], writes=[diagb])
        S.dma("sp", augl[:], din["augl"], writes=[augl])
        S.dma("sp", augr[:], din["augr"], writes=[augr])
        self.row_to_cols(wcol, wcol[:, 0:16], din["nmix"][layer], 16)
        self.row_to_cols(subcol, subcol[:, 0:2], din["subw"][j], 2)
        S.op("dve", lambda e: e.tensor_scalar(subcol[:, :], subcol[:, :], 1.0 - lam_init, None, ALU.mult), reads=[subcol], writes=[subcol])
        self.bcast_row(lq, din["lamq"][j], 256)
        self.bcast_row(lk, din["lamk"][j], 256)
        S.op("dve", lambda e: e.tensor_tensor(lq[:, :], lq[:, :], lk[:, :], ALU.mult), reads=[lq, lk], writes=[lq])
        S.op("dve", lambda e: e.tensor_reduce(lsum[:, 0:2], lq[:, :].rearrange("p (a b) -> p a b", a=2), axis=AX.X, op=ALU.add), reads=[lq], writes=[lsum])
        S.op("act", lambda e: e.activation(lsum[:, :], lsum[:, :], AF.Exp), reads=[lsum], writes=[lsum])
        S.op("dve", lambda e: e.tensor_tensor(neglam[:, :], lsum[:, 1:2], lsum[:, 0:1], ALU.subtract), reads=[lsum], writes=[neglam])
        S.op("dve", lambda e: e.tensor_scalar(neglam[:, :], neglam[:, :], -lam_init, None, ALU.add), reads=[neglam], writes=[neglam])
        S.op("dve", lambda e: e.memset(Vext[:, :, 256:257], 1.0), writes=[Vext])

        ctr = {"e": 0, "p": 0, "o": 0}
        for blk in self.blocks(512):
            T, rows, nt = blk.T, blk.rows, blk.nt
            isp = blk.kind == "p"
            self.convert_some(8)
            if float(os.environ.get("ASTOP", "99")) <= 0.5:
                continue
            self.prep_xnT(blk, src_idx, wcol, xnT, hin, xs_all, stat)
            if float(os.environ.get("ASTOP", "99")) <= 0.7:
                continue

            def qkv_ep(nb, t, bank, ncol):
                i = ctr["e"]
                ctr["e"] += 1
                c0 = (nb % 4) * 512
                if isp:
                    r = blk.r0 + t * 128
                    tile_buf = self.kvscr_buf[r // 128]
                if nb < 4 or nb < 8:
                    kb = kbf[i % 2]
                    kts = kTst[i % 2]
                    if nb < 4:
                        S.op("act", lambda e: e.activation(kb[:rows, :], bank[:rows, 0:512], AF.Copy, scale=float(128 ** -0.5)), reads=[bank], writes=[kb])
                    else:
                        ks = kst[i % 2]
                        S.op("dve", lambda e: e.tensor_copy(ks[:rows, :], bank[:rows, 0:512]), reads=[bank], writes=[ks])
                        S.op("act", lambda e: e.activation(kb[:rows, :], ks[:rows, :], AF.Copy), reads=[ks], writes=[kb])
                        if isp:
                            dst = self.dout["k_p"][j][r:r + 128, c0:c0 + 512]
                        else:
                            dst = self.dout["k_s"][j, blk.si][:, c0:c0 + 512]
                        S.dma("sp", dst, ks[:rows, :], reads=[ks], writes=[self.outbuf])
                    bankT = self.psum_next()
                    bvT = bankT[:].bitcast(BF16)
                    for ii in range(4):
                        S.op("pe", lambda e, ii=ii: e.transpose(bvT[:, ii * 128:ii * 128 + rows], kb[:rows, ii * 128:(ii + 1) * 128], self.ident_bf[:rows, :rows]),
                             reads=[kb, self.ident_bf], writes=[bankT])
                    src3 = bvT[:, 0:512].rearrange("p (i r) -> p i r", i=4)[:, :, 0:rows]
                    sg0 = (nb % 4) * 4
                    if nb < 4:
                        S.op("dve", lambda e: e.tensor_copy(qT[:, sg0:sg0 + 4, t * rows:(t + 1) * rows], src3), reads=[bankT], writes=[qT])
                    elif isp:
                        S.op("dve", lambda e: e.tensor_copy(kts[:, :, 0:rows], src3), reads=[bankT], writes=[kts])
                        S.dma("sp", self.kT_scr[sg0:sg0 + 4, :, r:r + 128].rearrange("s p c -> p s c"), kts[:, :, 0:rows], reads=[kts], writes=[tile_buf])
                    else:
                        S.op("dve", lambda e: e.tensor_copy(kTnew[:, sg0:sg0 + 4, 0:rows], src3), reads=[bankT], writes=[kTnew])
                else:
                    ks = kst[i % 2]
                    S.op("dve", lambda e: e.tensor_copy(ks[:rows, :], bank[:rows, 0:512]), reads=[bank], writes=[ks])
                    if isp:
                        dst = self.dout["v_p"][j][r:r + 128, c0:c0 + 512]
                    else:
                        dst = self.dout["v_s"][j, blk.si][:, c0:c0 + 512]
                    S.dma("sp", dst, ks[:rows, :], reads=[ks], writes=[self.outbuf])
                    if isp:
                        kb = kbf[i % 2]
                        S.op("act", lambda e: e.activation(kb[:rows, :], ks[:rows, :], AF.Copy), reads=[ks], writes=[kb])
                        S.dma("sp", self.v_scr[r:r + 128, c0:c0 + 512], kb[:rows, :], reads=[kb], writes=[tile_buf])
                    else:
                        S.op("act", lambda e: e.activation(vnew[:rows, c0:c0 + 512], ks[:rows, :], AF.Copy), reads=[ks], writes=[vnew])
            self.gemm_b(blk, lambda t, c: xnT[:, c, t * rows:(t + 1) * rows], [xnT], "qkv", j, range(12), 1, qkv_ep)

            ASTOP = float(os.environ.get("ASTOP", "99"))
            if ASTOP <= 1:
                continue
            if isp:
                b4 = blk.r0 // 128
                nkt = b4 + nt
                nk = nkt * 128
                kv_reads = self.kvscr_buf[0:nkt]
            for hd in range(AH):
                if isp:
                    for m in range(2):
                        S.dma("sp", KT[:, m, 0:nk], self.kT_scr[hd * 2 + m][:, 0:nk], reads=kv_reads, writes=[KT])
                    S.dma("sp", Vext[:, 0:nkt, 0:256], self.v_scr[0:nk, hd * 256:(hd + 1) * 256].rearrange("(j p) e -> p j e", p=128), reads=kv_reads, writes=[Vext])
                    tiles = []
                    for kj in range(nkt):
                        if kj < b4:
                            tiles.append((kj, 128, "aug", b4 - kj, 0))
                        else:
                            tiles.append((kj, 128, "diag", 0, kj - b4))
                else:
                    S.dma("pool", ckst[:, 0:NPT, :], din["ck"][j, blk.si][:, hd * 256:(hd + 1) * 256].rearrange("(j p) e -> p j e", p=128), reads=[], writes=[ckst])
                    S.dma("pool", Vext[:, 0:NPT, 0:256], din["cv"][j, blk.si][:, hd * 256:(hd + 1) * 256].rearrange("(j p) e -> p j e", p=128), reads=[], writes=[Vext])
                    for kj0 in range(0, NPT, 4):
                        nkk = min(4, NPT - kj0)
                        bankT = self.psum_next()
                        bvT = bankT[:].bitcast(BF16)
                        for m in range(2):
                            for q in range(nkk):
                                S.op("pe", lambda e, m=m, q=q: e.transpose(bvT[:, m * 512 + q * 128:m * 512 + (q + 1) * 128], ckst[:, kj0 + q, m * 128:(m + 1) * 128], self.ident_bf[:, :]),
                                     reads=[ckst, self.ident_bf], writes=[bankT])
                        for m in range(2):
                            S.op("dve", lambda e, m=m: e.tensor_copy(KT[:, m, kj0 * 128:(kj0 + nkk) * 128], bvT[:, m * 512:m * 512 + nkk * 128]), reads=[bankT], writes=[KT])
                    for m in range(2):
                        S.op("dve", lambda e, m=m: e.tensor_copy(KT[:, m, PAST:PAST + DS], kTnew[:, hd * 2 + m, 0:DS]), reads=[kTnew], writes=[KT])
                    S.op("dve", lambda e: e.tensor_copy(Vext[0:DS, NPT, 0:256], vnew[0:DS, hd * 256:(hd + 1) * 256]), reads=[vnew], writes=[Vext])
                    tiles = [(kj, 128, "aug", NPT - kj, 0) for kj in range(NPT)] + [(NPT, DS, "diag", 0, 0)]

                if ASTOP <= 2:
                    continue
                for m in range(2):
                    sg = hd * 2 + m
                    accs = [self.psum_next(hold=True) for _ in range(nt)]
                    def emit_qk(idx):
                        kj, kr, mode, d0, t0 = tiles[idx]
                        ncols = (nt - t0) * rows
                        bankS = self.psum_next()
                        S.op("pe", lambda e: e.matmul(bankS[0:kr, 0:ncols], KT[:, m, kj * 128:kj * 128 + kr], qT[:, sg, t0 * rows:nt * rows], start=True, stop=False),
                             reads=[KT, qT], writes=[bankS])
                        if mode == "diag":
                            S.op("pe", lambda e: e.matmul(bankS[0:kr, 0:rows], self.ident_bf[0:kr, 0:kr], diagb[0:kr, hd, 0:rows], start=False, stop=(ncols == rows)),
                                 reads=[self.ident_bf, diagb], writes=[bankS])
                            if ncols > rows:
                                S.op("pe", lambda e: e.matmul(bankS[0:kr, rows:ncols], augl[:, hd, 0:kr], augr[:, 128:128 + ncols - rows], start=False, stop=True),
                                     reads=[augl, augr], writes=[bankS])
                        else:
                            S.op("pe", lambda e: e.matmul(bankS[0:kr, 0:ncols], augl[:, hd, 0:kr], augr[:, d0 * 128:d0 * 128 + ncols], start=False, stop=True),
                                 reads=[augl, augr], writes=[bankS])
                        return bankS
                    LOOK = 2
                    pend = [emit_qk(i) for i in range(min(LOOK, len(tiles)))]
                    for idx, (kj, kr, mode, d0, t0) in enumerate(tiles):
                        ncols = (nt - t0) * rows
                        bankS = pend.pop(0)
                        if idx + LOOK < len(tiles):
                            pend.append(emit_qk(idx + LOOK))
                        p = pt[ctr["p"] % 3]
                        ctr["p"] += 1
                        S.op("act", lambda e: e.activation(p[0:kr, 0:ncols], bankS[0:kr, 0:ncols], AF.Exp), reads=[bankS], writes=[p])
                        for t in range(t0, nt if ASTOP > 3 else 0):
                            lastk = (kj == b4 + t) if isp else (idx == len(tiles) - 1)
                            S.op("pe", lambda e, t=t: e.matmul(accs[t][0:rows, 0:257], p[0:kr, (t - t0) * rows:(t - t0 + 1) * rows], Vext[0:kr, kj, 0:257], start=(idx == 0), stop=lastk),
                                 reads=[p, Vext], writes=[accs[t]])
                    for t in range(nt if ASTOP > 4 else 0):
                        S.op("dve", lambda e, t=t: e.reciprocal(rsum[:rows, t:t + 1], accs[t][0:rows, 256:257]), reads=[accs[t]], writes=[rsum])
                        if m == 0:
                            S.op("dve", lambda e, t=t: e.tensor_scalar(o_sb[:rows, t, :], accs[t][0:rows, 0:256], rsum[:rows, t:t + 1], None, ALU.mult), reads=[accs[t], rsum], writes=[o_sb])
                        else:
                            S.op("dve", lambda e, t=t: e.tensor_tensor(rs2[:rows, t:t + 1], rsum[:rows, t:t + 1], neglam[:rows, :], ALU.mult), reads=[rsum, neglam], writes=[rs2])
                            S.op("dve", lambda e, t=t: e.scalar_tensor_tensor(o_sb[:rows, t, :], accs[t][0:rows, 0:256], rs2[:rows, t:t + 1], o_sb[:rows, t, :], ALU.mult, ALU.add),
                                 reads=[accs[t], rs2, o_sb], writes=[o_sb])
                    self.psum_release(accs)
                for t in range(nt if ASTOP > 5 else 0):
                    ob = os_bf[ctr["o"] % 2]
                    ctr["o"] += 1
                    S.op("act", lambda e, t=t: e.activation(osq[:rows, :], o_sb[:rows, t, :], AF.Square), reads=[o_sb], writes=[osq])
                    S.op("dve", lambda e, t=t: e.reduce_sum(ss[:rows, t:t + 1], osq[:rows, :], axis=AX.X), reads=[osq], writes=[ss])
                    S.op("dve", lambda e, t=t: e.tensor_scalar(ss[:rows, t:t + 1], ss[:rows, t:t + 1], 1.0 / 256, EPS, ALU.mult, ALU.add), reads=[ss], writes=[ss])
                    S.op("act", lambda e, t=t: e.activation(ss[:rows, t:t + 1], ss[:rows, t:t + 1], AF.Sqrt), reads=[ss], writes=[ss])
                    S.op("dve", lambda e, t=t: e.reciprocal(ss[:rows, t:t + 1], ss[:rows, t:t + 1]), reads=[ss], writes=[ss])
                    S.op("act", lambda e, t=t: e.activation(ob[:rows, :], o_sb[:rows, t, :], AF.Copy, scale=ss[:rows, t:t + 1]), reads=[o_sb, ss], writes=[ob])
                    bankT = self.psum_next()
                    bvT = bankT[:].bitcast(BF16)
                    for half in range(2):
                        S.op("pe", lambda e, half=half: e.transpose(bvT[:, half * 128:half * 128 + rows], ob[:rows, half * 128:(half + 1) * 128], self.ident_bf[:rows, :rows]),
                             reads=[ob, self.ident_bf], writes=[bankT])
                    for half in range(2):
                        S.op("dve", lambda e, half=half, t=t: e.tensor_scalar(oT[:, hd * 2 + half, t * rows:(t + 1) * rows], bvT[:, half * 128:half * 128 + rows], subcol[:, half:half + 1], None, ALU.mult),
                             reads=[bankT, subcol], writes=[oT])

            ep = self.make_residual_epilogue(blk, src_idx, dst_idx, rin, rout)
            self.gemm_b(blk, lambda t, c: oT[:, c, t * rows:(t + 1) * rows], [oT], "wo", j, range(4), 1, ep)
        S.barrier()


Prog.att_pieces = att_pieces
Prog.att_pass = att_pass


def _add_mixers(prog):
    pass
```

```python
import math
import os
from contextlib import ExitStack

import numpy as np
import ml_dtypes
import concourse.bass as bass
import concourse.mybir as mybir
from concourse.bass_utils import run_bass_kernel_spmd

F32 = mybir.dt.float32
BF16 = mybir.dt.bfloat16
AF = mybir.ActivationFunctionType
ALU = mybir.AluOpType
AX = mybir.AxisListType

D = 2048
DFF = 8192
DIN = 4096
NH = 64
HP = 64
NG = 8
DST = 128
CONVD = 6144
SSD_IN = 10304
AH = 8
DS = 16
EPS = 1e-5
NRING = 8
NWB = 3


class Cfg:
    def __init__(self, SEQ=4096, PAST=2048, passes=None, depth=4):
        self.SEQ = SEQ
        self.PAST = PAST
        self.depth = depth
        self.passes = passes


class Buf:
    __slots__ = ("w", "r", "name")

    def __init__(self, name=""):
        self.w = None
        self.r = {}
        self.name = name


class Tile:
    def __init__(self, t, name=""):
        self.t = t
        self.b = Buf(name)

    def __getitem__(self, k):
        return self.t[k]


class PEProxy:
    def __init__(self, pe):
        self.pe = pe
        self.mode = None

    @staticmethod
    def _r(v):
        return 32 if v <= 32 else (64 if v <= 64 else 128)

    def _switch(self, lhsT):
        sh = lhsT.shape
        m = (self._r(sh[0]), self._r(int(np.prod(sh[1:]))))
        if m != self.mode:
            if self.mode is not None:
                self.pe.drain()
            self.mode = m

    def matmul(self, out, lhsT, rhs, **kw):
        self._switch(lhsT)
        return self.pe.matmul(out, lhsT, rhs, **kw)

    def transpose(self, out, in_, identity):
        self._switch(in_)
        return self.pe.transpose(out, in_, identity)

    def wait_ge(self, sem, val):
        return self.pe.wait_ge(sem, val)


class Sync:
    def __init__(self, nc, stack):
        self.nc = nc
        self.eng = {"pe": PEProxy(nc.tensor), "act": nc.scalar, "dve": nc.vector, "pool": nc.gpsimd, "sp": nc.sync}
        self.semobj = {}
        self.cnt = {}
        self.seen = {}
        for k in self.eng:
            self.semobj[("e", k)] = stack.enter_context(nc.semaphore("s_" + k))
            self.cnt[k] = 0
            self.seen[k] = {}
        self.ring_val = {}
        self.ring_pos = {}
        for q in ("sp", "pool"):
            self.ring_val[q] = [0] * NRING
            self.ring_pos[q] = 0
            for i in range(NRING):
                self.semobj[("d", q, i)] = stack.enter_context(nc.semaphore(f"d_{q}{i}"))
        sw = not os.environ.get("NOSELF")
        self.same_engine_wait = {"pe": False, "act": sw, "dve": sw, "pool": sw, "sp": False}
        self.nops = 0

    def _wait(self, e, tok):
        key, val = tok
        if self.seen[e].get(key, 0) >= val:
            return
        if key == ("e", e) and not self.same_engine_wait[e]:
            return
        self.eng[e].wait_ge(self.semobj[key], val)
        self.seen[e][key] = val

    def _deps(self, e, reads, writes):
        for b in reads:
            if b.w is not None:
                self._wait(e, b.w)
        for b in writes:
            if b.w is not None:
                self._wait(e, b.w)
            for k, v in b.r.items():
                self._wait(e, (k, v))

    @staticmethod
    def _bufs(xs):
        return [x.b if isinstance(x, Tile) else x for x in xs]

    def _commit(self, tok, reads, writes):
        for b in reads:
            b.r[tok[0]] = tok[1]
        for b in writes:
            b.w = tok
            b.r = {}

    def op(self, e, fn, reads=(), writes=()):
        reads = self._bufs(reads)
        writes = self._bufs(writes)
        self._deps(e, reads, writes)
        ins = fn(self.eng[e])
        self.cnt[e] += 1
        ins.then_inc(self.semobj[("e", e)], 1)
        self._commit((("e", e), self.cnt[e]), reads, writes)
        self.nops += 1
        return ins

    def dma(self, q, out, in_, reads=(), writes=(), **kw):
        reads = self._bufs(reads)
        writes = self._bufs(writes)
        self._deps(q, reads, writes)
        i = self.ring_pos[q]
        self.ring_pos[q] = (i + 1) % NRING
        key = ("d", q, i)
        prev = self.ring_val[q][i]
        if prev > 0:
            self._wait(q, (key, prev))
        val = prev + 16
        self.ring_val[q][i] = val
        self.eng[q].dma_start(out=out, in_=in_, **kw).then_inc(self.semobj[key], 16)
        self._commit((key, val), reads, writes)
        self.nops += 1

    def all_tokens(self):
        toks = []
        for k in self.eng:
            if self.cnt[k] > 0:
                toks.append((("e", k), self.cnt[k]))
        for q in self.ring_val:
            for i, v in enumerate(self.ring_val[q]):
                if v > 0:
                    toks.append((("d", q, i), v))
        return toks

    def barrier(self, engines=None):
        toks = self.all_tokens()
        for e in (engines or list(self.eng)):
            for t in toks:
                key, val = t
                if self.seen[e].get(key, 0) >= val:
                    continue
                if key == ("e", e):
                    if e in ("sp",):
                        continue
                self.eng[e].wait_ge(self.semobj[key], val)
                self.seen[e][key] = val


class Blk:
    def __init__(self, kind, r0, nt, rows, si=0):
        self.kind = kind
        self.r0 = r0
        self.nt = nt
        self.rows = rows
        self.si = si
        self.T = nt * rows


class Prog:
    def __init__(self, cfg):
        self.cfg = cfg
        self.nc = bass.Bass("TRN2", target_bir_lowering=False)
        self.stack = ExitStack()
        self.S = Sync(self.nc, self.stack)
        self.din = {}
        self.dout = {}
        self.declare()

    def inp(self, name, shape, dt=F32):
        self.din[name] = self.nc.dram_tensor(name, list(shape), dt, kind="ExternalInput").ap()
        return self.din[name]

    def outp(self, name, shape, dt=F32):
        self.dout[name] = self.nc.dram_tensor(name, list(shape), dt, kind="ExternalOutput").ap()
        return self.dout[name]

    def scr(self, name, shape, dt):
        return self.nc.dram_tensor(name, list(shape), dt).ap()

    def declare(self):
        c = self.cfg
        SEQ, PAST = c.SEQ, c.PAST
        i = self.inp
        i("xp", [SEQ, D]); i("xs", [2, DS, D])
        i("ck", [2, 2, PAST, 2048]); i("cv", [2, 2, PAST, 2048])
        i("sssm", [2, 2, NH * HP, DST]); i("sconv", [2, 2, 3, CONVD])
        i("nmix", [4, D]); i("nmlp", [4, D]); i("fnw", [1, D])
        i("w_in", [2, D, SSD_IN]); i("convw", [2, 4, CONVD]); i("convb", [2, 1, CONVD])
        i("dtb", [2, NH]); i("alog", [2, NH]); i("dsk", [2, NH]); i("snw", [2, DIN])
        i("w_out", [2, DIN, D]); i("wqkv", [2, D, 6144]); i("lamq", [2, 256]); i("lamk", [2, 256])
        i("subw", [2, 256]); i("wo", [2, D, D]); i("wup", [4, D, DFF]); i("wdn", [4, DFF, D])
        i("ident_bf", [128, 128], BF16); i("ident_f", [128, 128]); i("tri_f", [128, 128]); i("ones_f", [128, 128])
        i("onehot_bf", [128, 64, 128], BF16)
        self.NA = SEQ + 1024
        i("augl", [128, AH, 128], BF16); i("augr", [128, self.NA], BF16); i("diagb", [128, AH, 128], BF16)
        o = self.outp
        o("y_p", [SEQ, D]); o("y_s", [2, DS, D])
        o("k_p", [2, SEQ, 2048]); o("v_p", [2, SEQ, 2048])
        o("ssm_p", [2, NH * HP, DST]); o("conv_p", [2, 3, CONVD])
        o("k_s", [2, 2, DS, 2048]); o("v_s", [2, 2, DS, 2048])
        o("ssm_s", [2, 2, NH * HP, DST]); o("conv_s", [2, 2, 3, CONVD])
        self.hP = [self.scr(f"hP{j}", [SEQ, D], F32) for j in range(2)]
        self.hS = [self.scr(f"hS{j}", [2, DS, D], F32) for j in range(2)]
        self.kT_scr = self.scr("kT_scr", [16, 128, SEQ], BF16)
        self.v_scr = self.scr("v_scr", [SEQ, 2048], BF16)
        self.wspec = {
            "ssd_in": ("w_in", 2, D, SSD_IN), "ssd_out": ("w_out", 2, DIN, D),
            "qkv": ("wqkv", 2, D, 6144), "wo": ("wo", 2, D, D),
            "up": ("wup", 4, D, DFF), "down": ("wdn", 4, DFF, D),
        }
        self.pid = {}
        self.wscr = []
        n = 0
        for name, (_, nl, K, N) in self.wspec.items():
            for l in range(nl):
                cnt = (K // 2048) * ((N + 511) // 512)
                ten = self.scr(f"wscr_{name}{l}", [cnt, 128, 16, 512], BF16)
                j = 0
                for kg in range(K // 2048):
                    for nb in range((N + 511) // 512):
                        self.pid[(name, l, kg, nb)] = n
                        self.pcols = getattr(self, "pcols", [])
                        self.pcols.append(min(512, N - nb * 512))
                        self.wscr.append(ten[j])
                        n += 1
                        j += 1
        self.npieces = n
        self.wscr_buf = [Buf(f"wscr{j}") for j in range(n)]
        nt128 = SEQ // 128
        self.hbuf = [[Buf() for _ in range(nt128)] for _ in range(2)]
        self.hsbuf = [[Buf() for _ in range(2)] for _ in range(2)]
        self.kvscr_buf = [Buf() for _ in range(nt128)]
        self.outbuf = Buf("outs")

    def piece_cols(self, name, nb):
        N = self.wspec[name][3]
        return min(512, N - nb * 512)

    def sb(self, stack, name, shape, dt):
        self._uid = getattr(self, "_uid", 0) + 1
        name = f"sb{self._uid}_{name}"
        return Tile(stack.enter_context(self.nc.sbuf_tensor(name, list(shape), dt)), name)

    def psum_next(self, hold=False):
        held = getattr(self, "psum_held", None)
        if held is None:
            held = self.psum_held = set()
        while True:
            i = self.psum_i
            self.psum_i = (self.psum_i + 1) % len(self.psum)
            if i not in held:
                break
        if hold:
            held.add(i)
        return self.psum[i]

    def psum_release(self, banks):
        for b in banks:
            self.psum_held.discard(self.psum.index(b))

    def hsrc(self, idx, blk, t):
        if blk.kind == "p":
            r = blk.r0 + t * 128
            if idx < 0:
                return self.din["xp"][r:r + 128, :], Buf()
            return self.hP[idx][r:r + 128, :], self.hbuf[idx][r // 128]
        if blk.kind == "m":
            if idx < 0:
                return self.din["xs"].rearrange("s t d -> (s t) d"), Buf()
            return self.hS[idx].rearrange("s t d -> (s t) d"), self.hsbuf[idx][0]
        if idx < 0:
            return self.din["xs"][blk.si], Buf()
        return self.hS[idx][blk.si], self.hsbuf[idx][blk.si]

    def blocks(self, T, merge=False):
        bl = [Blk("p", r, T // 128, 128) for r in range(0, self.cfg.SEQ, T)]
        if merge:
            bl += [Blk("m", 0, 1, 2 * DS, 0)]
        else:
            bl += [Blk("s", 0, 1, DS, 0), Blk("s", 0, 1, DS, 1)]
        return bl

    def emit_weight_conversion(self, order):
        S = self.S
        for (name, l) in order:
            dname, nl, K, N = self.wspec[name]
            W = self.din[dname][l]
            for kg in range(K // 2048):
                for nb in range((N + 511) // 512):
                    ncol = self.piece_cols(name, nb)
                    p = self.pid[(name, l, kg, nb)]
                    src = W[kg * 2048:(kg + 1) * 2048, nb * 512:nb * 512 + ncol].rearrange("(c p) n -> p c n", p=128)
                    dst = self.wscr[p][:, :, 0:ncol]
                    self.conv_queue.append((dst, src, p))

    def convert_some(self, n):
        for _ in range(min(n, len(self.conv_queue))):
            dst, src, p = self.conv_queue.pop(0)
            self.S.dma("pool", dst, src, reads=[], writes=[self.wscr_buf[p]])

    def wget(self, p):
        S = self.S
        if os.environ.get("KSTOP"):
            wb = self.wbuf[self.w_i % NWB]
            self.w_i += 1
            nco = self.pcols[p]
            S.dma("sp", wb[:, :, 0:nco], self.wscr[p][:, :, 0:nco], reads=[self.wscr_buf[p]], writes=[wb])
            return wb
        assert self.wsched[self.w_i] == p, (self.w_i, self.wsched[self.w_i], p)
        while self.w_loaded < min(len(self.wsched), self.w_i + NWB):
            j = self.w_loaded
            pj = self.wsched[j]
            wb = self.wbuf[j % NWB]
            nco = self.pcols[pj]
            S.dma("sp", wb[:, :, 0:nco], self.wscr[pj][:, :, 0:nco], reads=[self.wscr_buf[pj]], writes=[wb])
            self.w_loaded += 1
        wb = self.wbuf[self.w_i % NWB]
        self.w_i += 1
        return wb

    def gemm_b(self, blk, lhsT_fn, lhsT_bufs, name, layer, nbs, nkg, epilogue):
        S = self.S
        for nb in nbs:
            ncol = self.piece_cols(name, nb)
            banks = [self.psum_next() for _ in range(blk.nt)]
            for kg in range(nkg):
                wb = self.wget(self.pid[(name, layer, kg, nb)])
                for t in range(blk.nt):
                    for cc in range(16):
                        first = (kg == 0 and cc == 0)
                        last = (kg == nkg - 1 and cc == 15)
                        S.op("pe", lambda e, t=t, cc=cc, kg=kg: e.matmul(
                            banks[t][:blk.rows, 0:ncol], lhsT_fn(t, kg * 16 + cc), wb[:, cc, 0:ncol],
                            start=first, stop=last), reads=[wb] + lhsT_bufs, writes=[banks[t]])
            for t in range(blk.nt):
                epilogue(nb, t, banks[t], ncol)

    def gemm_a(self, blk, xnT, name, layer, nbs, epilogue):
        S = self.S
        T = blk.T
        for nb in nbs:
            ncol = self.piece_cols(name, nb)
            wb = self.wget(self.pid[(name, layer, 0, nb)])
            for m in range(ncol // 128):
                bank = self.psum_next()
                for cc in range(16):
                    S.op("pe", lambda e, cc=cc, m=m: e.matmul(
                        bank[:, 0:T], wb[:, cc, m * 128:(m + 1) * 128], xnT[:, cc, 0:T],
                        start=(cc == 0), stop=(cc == 15)), reads=[wb, xnT], writes=[bank])
                epilogue(nb, m, bank)

    def row_to_cols(self, dst_tile, dst_ap, src_row_ap, C):
        S = self.S
        tmp = self.rowtmp
        S.dma("sp", tmp[:C, :], src_row_ap.rearrange("(c p) -> c p", p=128), reads=[], writes=[tmp])
        bank = self.psum_next()
        S.op("pe", lambda e: e.transpose(bank[:, 0:C], tmp[:C, :], self.ident_f[:C, :C]), reads=[tmp, self.ident_f], writes=[bank])
        S.op("dve", lambda e: e.tensor_copy(dst_ap, bank[:, 0:C]), reads=[bank], writes=[dst_tile])

    def bcast_row(self, dst_tile, src_row_ap, n):
        self.S.dma("sp", dst_tile[:, 0:n], src_row_ap.partition_broadcast(128), reads=[], writes=[dst_tile])

    def prep_xnT(self, blk, src_idx, wcol, xnT, hin, xs_all, stat):
        S = self.S
        rows = blk.rows
        for t in range(blk.nt):
            ap, hb = self.hsrc(src_idx, blk, t)
            S.dma("sp", hin[:rows, :], ap, reads=[hb], writes=[hin])
            S.op("act", lambda e, t=t: e.activation(xs_all[:rows, t, :], hin[:rows, :], AF.Square), reads=[hin], writes=[xs_all])
            S.op("dve", lambda e, t=t: e.reduce_sum(stat[:rows, t:t + 1], xs_all[:rows, t, :], axis=AX.X), reads=[xs_all], writes=[stat])
            S.op("dve", lambda e, t=t: e.tensor_scalar(stat[:rows, t:t + 1], stat[:rows, t:t + 1], 1.0 / D, EPS, ALU.mult, ALU.add), reads=[stat], writes=[stat])
            S.op("act", lambda e, t=t: e.activation(stat[:rows, t:t + 1], stat[:rows, t:t + 1], AF.Sqrt), reads=[stat], writes=[stat])
            S.op("dve", lambda e, t=t: e.reciprocal(stat[:rows, t:t + 1], stat[:rows, t:t + 1]), reads=[stat], writes=[stat])
            S.op("act", lambda e, t=t: e.activation(xs_all[:rows, t, :], hin[:rows, :], AF.Copy, scale=stat[:rows, t:t + 1]), reads=[hin, stat], writes=[xs_all])
        for cc in range(16):
            bank = self.psum_next()
            bv = bank[:].bitcast(BF16)
            for t in range(blk.nt):
                S.op("pe", lambda e, t=t, cc=cc: e.transpose(bv[:, t * rows:(t + 1) * rows], xs_all[:rows, t, cc * 128:(cc + 1) * 128], self.ident_bf[:rows, :rows]),
                     reads=[xs_all, self.ident_bf], writes=[bank])
            if cc % 2 == 0:
                S.op("act", lambda e, cc=cc: e.activation(xnT[:, cc, 0:blk.T], bv[:, 0:blk.T], AF.Copy, scale=wcol[:, cc:cc + 1]), reads=[bank, wcol], writes=[xnT])
            else:
                S.op("dve", lambda e, cc=cc: e.tensor_scalar(xnT[:, cc, 0:blk.T], bv[:, 0:blk.T], wcol[:, cc:cc + 1], None, ALU.mult), reads=[bank, wcol], writes=[xnT])

    def make_residual_epilogue(self, blk, src_idx, dst_idx, rin, rout):
        S = self.S
        state = {"i": 0}

        def ep(nb, t, bank, ncol):
            i = state["i"]
            state["i"] += 1
            ri = rin[i % len(rin)]
            ro = rout[i % len(rout)]
            rows = blk.rows
            sap, sbuf_ = self.hsrc(src_idx, blk, t)
            dap, dbuf = self.hsrc(dst_idx, blk, t)
            S.dma("sp", ri[:rows, 0:ncol], sap[:, nb * 512:nb * 512 + ncol], reads=[sbuf_], writes=[ri])
            S.op("dve", lambda e: e.tensor_tensor(ro[:rows, 0:ncol], bank[:rows, 0:ncol], ri[:rows, 0:ncol], ALU.add), reads=[bank, ri], writes=[ro])
            S.dma("sp", dap[:, nb * 512:nb * 512 + ncol], ro[:rows, 0:ncol], reads=[ro], writes=[dbuf])
        return ep

    def mlp_pieces(self, layer):
        pl = [self.pid[("up", layer, 0, nb)] for nb in range(16)]
        for nb in range(4):
            pl += [self.pid[("down", layer, kg, nb)] for kg in range(4)]
        return pl

    def mlp_pass(self, layer, src_idx, dst_idx):
        S = self.S
        with ExitStack() as st:
            xnT = self.sb(st, "xnT", [128, 16, 512], BF16)
            hin = self.sb(st, "hin", [128, D], F32)
            xs_all = self.sb(st, "xs_all", [128, 4, D], BF16)
            stat = self.sb(st, "stat", [128, 4], F32)
            wcol = self.sb(st, "wcol", [128, 16], F32)
            hT = self.sb(st, "hT", [128, 64, 512], BF16)
            rl = [self.sb(st, f"rl{j}", [128, 512], F32) for j in range(2)]
            rin = [self.sb(st, f"rin{j}", [128, 512], F32) for j in range(3)]
            rout = [self.sb(st, f"rout{j}", [128, 512], F32) for j in range(3)]
            self.row_to_cols(wcol, wcol[:, 0:16], self.din["nmlp"][layer], 16)
            for blk in self.blocks(512, merge=True):
                T = blk.T
                self.convert_some(8)
                self.prep_xnT(blk, src_idx, wcol, xnT, hin, xs_all, stat)
                cnt = {"i": 0}

                def up_ep(nb, m, bank):
                    r = rl[cnt["i"] % 2]
                    cnt["i"] += 1
                    S.op("act", lambda e: e.activation(r[:, 0:T], bank[:, 0:T], AF.Relu), reads=[bank], writes=[r])
                    S.op("dve", lambda e: e.tensor_tensor(hT[:, nb * 4 + m, 0:T], r[:, 0:T], r[:, 0:T], ALU.mult), reads=[r], writes=[hT])
                self.gemm_a(blk, xnT, "up", layer, range(16), up_ep)
                ep = self.make_residual_epilogue(blk, src_idx, dst_idx, rin, rout)
                self.gemm_b(blk, lambda t, c: hT[:, c, t * blk.rows:(t + 1) * blk.rows], [hT], "down", layer, range(4), 4, ep)
            S.barrier()

    def final_pass(self, src_idx):
        S = self.S
        with ExitStack() as st:
            hin = [self.sb(st, f"fhin{j}", [128, D], F32) for j in range(2)]
            sq = self.sb(st, "fsq", [128, D], F32)
            ho = [self.sb(st, f"fho{j}", [128, D], F32) for j in range(2)]
            stat = self.sb(st, "fstat", [128, 2], F32)
            wrep = self.sb(st, "fwrep", [128, D], F32)
            self.bcast_row(wrep, self.din["fnw"][0], D)
            i = 0
            for blk in self.blocks(512):
                rows = blk.rows
                for t in range(blk.nt):
                    hi = hin[i % 2]; o = ho[i % 2]; j = i % 2
                    i += 1
                    ap, hb = self.hsrc(src_idx, blk, t)
                    S.dma("sp", hi[:rows, :], ap, reads=[hb], writes=[hi])
                    S.op("act", lambda e: e.activation(sq[:rows, :], hi[:rows, :], AF.Square), reads=[hi], writes=[sq])
                    S.op("dve", lambda e: e.reduce_sum(stat[:rows, j:j + 1], sq[:rows, :], axis=AX.X), reads=[sq], writes=[stat])
                    S.op("dve", lambda e: e.tensor_scalar(stat[:rows, j:j + 1], stat[:rows, j:j + 1], 1.0 / D, EPS, ALU.mult, ALU.add), reads=[stat], writes=[stat])
                    S.op("act", lambda e: e.activation(stat[:rows, j:j + 1], stat[:rows, j:j + 1], AF.Sqrt), reads=[stat], writes=[stat])
                    S.op("dve", lambda e: e.reciprocal(stat[:rows, j:j + 1], stat[:rows, j:j + 1]), reads=[stat], writes=[stat])
                    S.op("dve", lambda e: e.scalar_tensor_tensor(o[:rows, :], hi[:rows, :], stat[:rows, j:j + 1], wrep[:rows, :], ALU.mult, ALU.mult), reads=[hi, stat, wrep], writes=[o])
                    if blk.kind == "p":
                        r = blk.r0 + t * 128
                        dst = self.dout["y_p"][r:r + 128, :]
                    else:
                        dst = self.dout["y_s"][blk.si]
                    S.dma("sp", dst, o[:rows, :], reads=[o], writes=[self.outbuf])
            S.barrier()

    def pass_list(self):
        if self.cfg.passes is not None:
            return self.cfg.passes
        pl = []
        for i in range(self.cfg.depth):
            pl.append(("ssd" if i % 2 == 0 else "att", i))
            pl.append(("mlp", i))
        pl.append(("final", 0))
        return pl

    def pass_pieces(self, kind, layer, blk=None):
        if kind == "mlp":
            return self.mlp_pieces(layer)
        if kind == "ssd":
            return self.ssd_pieces(layer // 2)
        if kind == "att":
            return self.att_pieces(layer // 2)
        return []

    def pass_T(self, kind):
        return 256 if kind == "ssd" else 512

    def build(self):
        S = self.S
        st = self.stack
        nc = self.nc
        passes = self.pass_list()
        self.psum = [Tile(st.enter_context(nc.psum_tensor(f"ps{j}", [128, 512], F32)), f"ps{j}") for j in range(8)]
        self.psum_i = 0
        self.wbuf = [self.sb(st, f"wb{j}", [128, 16, 512], BF16) for j in range(NWB)]
        self.ident_bf = self.sb(st, "ident_bf", [128, 128], BF16)
        self.ident_f = self.sb(st, "ident_f", [128, 128], F32)
        self.rowtmp = self.sb(st, "rowtmp", [64, 128], F32)
        S.dma("sp", self.ident_bf[:], self.din["ident_bf"], writes=[self.ident_bf])
        S.dma("sp", self.ident_f[:], self.din["ident_f"], writes=[self.ident_f])
        self.wsched = []
        conv_order = []
        for kind, layer in passes:
            if kind == "final":
                continue
            per_blk = self.pass_pieces(kind, layer)
            nblk = len(self.blocks(self.pass_T(kind), merge=(kind == "mlp")))
            self.wsched += per_blk * nblk
            if kind == "mlp":
                conv_order += [("up", layer), ("down", layer)]
            elif kind == "ssd":
                conv_order += [("ssd_in", layer // 2), ("ssd_out", layer // 2)]
            elif kind == "att":
                conv_order += [("qkv", layer // 2), ("wo", layer // 2)]
        self.w_i = 0
        self.w_loaded = 0
        self.conv_queue = []
        self.emit_weight_conversion(conv_order)
        nfirst = 0
        for kind, layer in passes[:2]:
            if kind != "final":
                nfirst += len(self.pass_pieces(kind, layer))
        self.convert_some(nfirst)
        src = -1
        nxt = 0
        for kind, layer in passes:
            if kind == "final":
                self.final_pass(src)
                continue
            if kind == "mlp":
                self.mlp_pass(layer, src, nxt)
            elif kind == "ssd":
                self.ssd_pass(layer // 2, layer, src, nxt)
            elif kind == "att":
                self.att_pass(layer // 2, layer, src, nxt)
            src = nxt
            nxt = 1 - nxt
        S.barrier()
        self.stack.close()
        return nc


def host_tables(SEQ):
    bf = ml_dtypes.bfloat16
    t = {}
    t["ident_bf"] = np.eye(128, dtype=np.float32).astype(bf)
    t["ident_f"] = np.eye(128, dtype=np.float32)
    t["tri_f"] = np.triu(np.ones((128, 128), dtype=np.float32))
    t["ones_f"] = np.ones((128, 128), dtype=np.float32)
    oh = np.zeros((128, 64, 128), dtype=np.float32)
    for h in range(64):
        oh[h, h, :] = 1.0
    t["onehot_bf"] = oh.astype(bf)
    slopes = 2.0 ** (-8.0 * np.arange(1, AH + 1) / AH)
    kl = np.arange(128, dtype=np.float32)
    augl = np.zeros((128, AH, 128), dtype=np.float32)
    for h in range(AH):
        augl[0, h] = slopes[h] * kl
        augl[1, h] = slopes[h]
        augl[2, h] = slopes[h]
    t["augl"] = augl.astype(bf)
    NA = SEQ + 1024
    ii = np.arange(NA)
    augr = np.zeros((128, NA), dtype=np.float32)
    augr[0] = 1.0
    augr[1] = -128.0 * (ii // 128)
    augr[2] = -(ii % 128).astype(np.float32)
    t["augr"] = augr.astype(bf)
    k = np.arange(128)[:, None]
    q = np.arange(128)[None, :]
    vis = (k // 64) <= (q // 64)
    diag = np.zeros((128, AH, 128), dtype=np.float32)
    for h in range(AH):
        diag[:, h, :] = np.where(vis, -slopes[h] * np.abs(q - k), -30000.0)
    t["diagb"] = diag.astype(bf)
    return t


def make_in_maps(inputs, cfg, ncores):
    SEQ, PAST = cfg.SEQ, cfg.PAST
    tabs = host_tables(SEQ)
    f = lambda a: np.ascontiguousarray(np.asarray(a, dtype=np.float32))
    shared = {
        "nmix": f(inputs["norm_mix_w"]), "nmlp": f(inputs["norm_mlp_w"]), "fnw": f(inputs["final_norm_w"]).reshape(1, D),
        "w_in": f(inputs["ssd_w_in"]), "convw": f(inputs["ssd_conv_w"]), "convb": f(inputs["ssd_conv_b"]).reshape(2, 1, CONVD),
        "dtb": f(inputs["ssd_dt_bias"]), "alog": f(inputs["ssd_a_log"]), "dsk": f(inputs["ssd_d"]), "snw": f(inputs["ssd_norm_w"]),
        "w_out": f(inputs["ssd_w_out"]), "wqkv": f(inputs["att_w_qkv"]), "lamq": f(inputs["att_lam_q"]).reshape(2, 256),
        "lamk": f(inputs["att_lam_k"]).reshape(2, 256), "subw": f(inputs["att_subln_w"]), "wo": f(inputs["att_w_o"]),
        "wup": f(inputs["mlp_w_up"]), "wdn": f(inputs["mlp_w_down"]),
    }
    shared.update(tabs)
    maps = []
    xp = f(inputs["x_prompt"]); xs = f(inputs["x_sample"])
    ck = f(inputs["cache_k"]); cv = f(inputs["cache_v"])
    ssm = f(inputs["state_ssm"]); cvs = f(inputs["state_conv"])
    for c in range(ncores):
        m = dict(shared)
        m["xp"] = np.ascontiguousarray(xp[c, :SEQ])
        m["xs"] = np.ascontiguousarray(xs[2 * c:2 * c + 2])
        m["ck"] = np.ascontiguousarray(ck[:, 2 * c:2 * c + 2, :PAST].reshape(2, 2, PAST, 2048))
        m["cv"] = np.ascontiguousarray(cv[:, 2 * c:2 * c + 2, :PAST].reshape(2, 2, PAST, 2048))
        m["sssm"] = np.ascontiguousarray(ssm[:, 2 * c:2 * c + 2].reshape(2, 2, NH * HP, DST))
        m["sconv"] = np.ascontiguousarray(cvs[:, 2 * c:2 * c + 2])
        maps.append(m)
    return maps


_PROG_CACHE = {}


def run(inputs, cfg, ncores=8, trace=False):
    key = (cfg.SEQ, cfg.PAST, cfg.depth, str(cfg.passes))
    prog = Prog(cfg)
    _add_mixers(prog)
    nc = prog.build()
    maps = make_in_maps(inputs, cfg, ncores)
    res = run_bass_kernel_spmd(nc, maps, core_ids=list(range(ncores)), trace=trace)
    return res, prog


def kernel(**inputs):
    cfg = Cfg()
    res, prog = run(inputs, cfg, 8)
    r = res.results
    SEQ = cfg.SEQ
    y_p = np.stack([r[c]["y_p"] for c in range(8)])
    y_s = np.concatenate([r[c]["y_s"] for c in range(8)], axis=0)
    k_p = np.stack([r[c]["k_p"] for c in range(8)], axis=1).reshape(2, 8, SEQ, AH, 256)
    v_p = np.stack([r[c]["v_p"] for c in range(8)], axis=1).reshape(2, 8, SEQ, AH, 256)
    ssm_p = np.stack([r[c]["ssm_p"] for c in range(8)], axis=1).reshape(2, 8, NH, HP, DST)
    conv_p = np.stack([r[c]["conv_p"] for c in range(8)], axis=1)
    k_s = np.concatenate([r[c]["k_s"] for c in range(8)], axis=1).reshape(2, 16, DS, AH, 256)
    v_s = np.concatenate([r[c]["v_s"] for c in range(8)], axis=1).reshape(2, 16, DS, AH, 256)
    ssm_s = np.concatenate([r[c]["ssm_s"] for c in range(8)], axis=1).reshape(2, 16, NH, HP, DST)
    conv_s = np.concatenate([r[c]["conv_s"] for c in range(8)], axis=1)
    outs = (y_p, y_s, k_p, v_p, ssm_p, conv_p, k_s, v_s, ssm_s, conv_s)
    return tuple(np.ascontiguousarray(o, dtype=np.float32) for o in outs)


def ssd_pieces(self, j):
    pl = [self.pid[("ssd_in", j, 0, nb)] for nb in (16, 17, 18, 19, 20)]
    for g in range(8):
        pl += [self.pid[("ssd_in", j, 0, g)], self.pid[("ssd_in", j, 0, 8 + g)]]
    for nb in range(4):
        pl += [self.pid[("ssd_out", j, kg, nb)] for kg in range(2)]
    return pl


def ssd_pass(self, j, layer, src_idx, dst_idx):
    S = self.S
    din = self.din
    with ExitStack() as st:
        sb = lambda name, shape, dt: self.sb(st, name, shape, dt)
        xnT = sb("xnT", [128, 16, 256], BF16)
        hin = sb("hin", [128, D], F32)
        xs_all = sb("xs_all", [128, 2, D], BF16)
        stat = sb("stat", [128, 4], F32)
        wcol = sb("wcol", [128, 16], F32)
        tri = sb("tri", [128, 128], F32)
        tri_b = sb("tri_b", [128, 128], BF16)
        ones_b = sb("ones_b", [128, 128], BF16)
        av_hi = sb("av_hi", [128, 128], BF16)
        av_lo = sb("av_lo", [128, 128], BF16)
        onehot = sb("onehot", [128, 64, 128], BF16)
        cw = sb("cw", [128, 48, 4], F32)
        cb = sb("cb", [128, 48], F32)
        gnw = sb("gnw", [128, 32], F32)
        dtb_rep = sb("dtb_rep", [128, 64], F32)
        aneg = sb("aneg", [128, 64], F32)
        D_rep = sb("D_rep", [128, 64], F32)
        halo = sb("halo", [128, 48, 3], F32)
        tmpc = sb("tmpc", [128, 48], F32)
        state = sb("state", [128, DIN], F32)
        state_bf = sb("state_bf", [128, DIN], BF16)
        stload = sb("stload", [128, 8, 128], F32)
        BT = sb("BT", [128, 8, 256], BF16)
        CT = sb("CT", [128, 8, 256], BF16)
        B_tok = sb("B_tok", [128, 2, 1024], BF16)
        dtv = sb("dtv", [128, 2, 64], F32)
        lndt = sb("lndt", [128, 2, 64], F32)
        av = sb("av", [128, 2, 64], F32)
        tmp64 = sb("tmp64", [128, 64], F32)
        acum_sb = sb("acum_sb", [128, 2, 64], F32)
        biasS = sb("biasS", [128, 2, 64], F32)
        e_t = sb("e_t", [128, 2, 64], F32)
        tailw = sb("tailw", [128, 2, 64], F32)
        etot = sb("etot", [128, 2, 64], F32)
        acT_f = sb("acT_f", [128, 128], F32)
        acT_hi = sb("acT_hi", [128, 2, 128], BF16)
        acT_lo = sb("acT_lo", [128, 2, 128], BF16)
        cbm = sb("cbm", [128, 16, 128], F32)
        sz_g = sb("sz_g", [128, 2, 512], BF16)
        xT_stage = sb("xT_stage", [128, 4, 256], BF16)
        x_tok_g = sb("x_tok_g", [128, 2, 512], BF16)
        xD = [sb(f"xD{i}", [128, 512], BF16) for i in range(2)]
        stg = [sb(f"stg{i}", [128, 3 + 256], F32) for i in range(2)]
        acc = [sb(f"acc{i}", [128, 256], F32) for i in range(2)]
        dec = [sb(f"dec{i}", [128, 128], F32) for i in range(8)]
        MT = [sb(f"MT{i}", [128, 128], BF16) for i in range(16)]
        t1 = [sb(f"t1_{i}", [128, 512], F32) for i in range(2)]
        t2 = [sb(f"t2_{i}", [128, 512], F32) for i in range(2)]
        gsb = [sb(f"gsb{i}", [128, 512], F32) for i in range(2)]
        gs_bf = [sb(f"gs_bf{i}", [128, 512], BF16) for i in range(2)]
        gss = sb("gss", [128, 2], F32)
        xw = [sb(f"xw{i}", [128, 512], BF16) for i in range(2)]
        gT = sb("gT", [128, 32, 256], BF16)
        rin = [sb(f"rin{i}", [128, 512], F32) for i in range(2)]
        rout = [sb(f"rout{i}", [128, 512], F32) for i in range(2)]

        S.dma("sp", tri[:], din["tri_f"], writes=[tri])
        S.op("dve", lambda e: e.tensor_copy(tri_b[:, :], tri[:, :]), reads=[tri], writes=[tri_b])
        S.op("dve", lambda e: e.memset(ones_b[:, :], 1.0), writes=[ones_b])
        S.dma("sp", onehot[:], din["onehot_bf"], writes=[onehot])
        self.row_to_cols(wcol, wcol[:, 0:16], din["nmix"][layer], 16)
        for jj in range(4):
            self.row_to_cols(cw, cw[:, :, jj], din["convw"][j, jj], 48)
        self.row_to_cols(cb, cb[:, 0:48], din["convb"][j, 0], 48)
        self.row_to_cols(gnw, gnw[:, 0:32], din["snw"][j], 32)
        self.bcast_row(dtb_rep, din["dtb"][j], 64)
        self.bcast_row(D_rep, din["dsk"][j], 64)
        self.bcast_row(aneg, din["alog"][j], 64)
        S.op("act", lambda e: e.activation(aneg[:, :], aneg[:, :], AF.Exp), reads=[aneg], writes=[aneg])
        S.op("dve", lambda e: e.tensor_scalar(aneg[:, :], aneg[:, :], -1.0, None, ALU.mult), reads=[aneg], writes=[aneg])

        ctr = {"conv": 0, "k": 0}
        KSTOP = float(os.environ.get("KSTOP", "99"))
        blocks = self.blocks(256) if KSTOP > 1 else []
        for bi, blk in enumerate(blocks):
            T, rows, nt = blk.T, blk.rows, blk.nt
            CL = rows
            self.convert_some(4)
            first = (bi == 0) or (blocks[bi - 1].kind != blk.kind) or (blocks[bi - 1].si != blk.si)
            last = (bi == len(blocks) - 1) or (blocks[bi + 1].kind != blk.kind) or (blocks[bi + 1].si != blk.si)
            if blk.kind == "p":
                conv_out = self.dout["conv_p"][j]
                ssm_out = self.dout["ssm_p"][j]
            else:
                conv_out = self.dout["conv_s"][j, blk.si]
                ssm_out = self.dout["ssm_s"][j, blk.si]
            if first:
                if blk.kind == "p":
                    S.op("dve", lambda e: e.memset(state[:, :], 0.0), writes=[state])
                    S.op("dve", lambda e: e.memset(state_bf[:, :], 0.0), writes=[state_bf])
                    S.op("dve", lambda e: e.memset(halo[:, :, :], 0.0), writes=[halo])
                else:
                    src = din["sssm"][j, blk.si].rearrange("(q r) n -> r q n", r=128)
                    for qt in range(4):
                        S.dma("sp", stload[:, :, :], src[:, qt * 8:(qt + 1) * 8, :], writes=[stload])
                        for q4 in range(2):
                            bank = self.psum_next()
                            for qq in range(4):
                                q = q4 * 4 + qq
                                S.op("pe", lambda e, q=q, qq=qq: e.transpose(bank[:, qq * 128:(qq + 1) * 128], stload[:, q, :], self.ident_f[:, :]),
                                     reads=[stload, self.ident_f], writes=[bank])
                            c0 = (qt * 8 + q4 * 4) * 128
                            S.op("dve", lambda e, c0=c0: e.tensor_copy(state[:, c0:c0 + 512], bank[:, 0:512]), reads=[bank], writes=[state])
                    S.op("act", lambda e: e.activation(state_bf[:, :], state[:, :], AF.Copy), reads=[state], writes=[state_bf])
                    for jj in range(3):
                        self.row_to_cols(halo, halo[:, :, jj], din["sconv"][j, blk.si, jj], 48)

            self.prep_xnT(blk, src_idx, wcol, xnT, hin, xs_all, stat)
            if KSTOP <= 2:
                continue

            def conv_ep(cc, bank, dest_tile, dest_ap):
                i = ctr["conv"]
                ctr["conv"] += 1
                sg = stg[i % 2]
                ac = acc[i % 2]
                S.op("act", lambda e: e.activation(sg[:, 3:3 + T], bank[:, 0:T], AF.Copy), reads=[bank], writes=[sg])
                S.op("dve", lambda e: e.tensor_copy(sg[:, 0:3], halo[:, cc, :]), reads=[halo], writes=[sg])
                S.op("dve", lambda e: e.tensor_copy(halo[:, cc, :], sg[:, T:T + 3]), reads=[sg], writes=[halo])
                S.op("act", lambda e: e.activation(ac[:, 0:T], sg[:, 0:T], AF.Identity, bias=cb[:, cc:cc + 1], scale=cw[:, cc, 0:1]),
                     reads=[sg, cb, cw], writes=[ac])
                for jj in (1, 2, 3):
                    S.op("dve", lambda e, jj=jj: e.scalar_tensor_tensor(ac[:, 0:T], sg[:, jj:jj + T], cw[:, cc, jj:jj + 1], ac[:, 0:T], ALU.mult, ALU.add),
                         reads=[sg, cw, ac], writes=[ac])
                S.op("act", lambda e: e.activation(dest_ap, ac[:, 0:T], AF.Silu), reads=[ac], writes=[dest_tile])

            def bc_ep(nb, m, bank):
                cc = (nb - 8) * 4 + m
                gg = (cc - 32) % 8
                if cc < 40:
                    conv_ep(cc, bank, BT, BT[:, gg, 0:T])
                else:
                    conv_ep(cc, bank, CT, CT[:, gg, 0:T])
            self.gemm_a(blk, xnT, "ssd_in", j, (16, 17, 18, 19), bc_ep)
            if KSTOP <= 3:
                continue

            def dt_ep(nb, t, bank, ncol):
                S.op("dve", lambda e: e.tensor_tensor(dtv[:rows, t, :], bank[:rows, 0:64], dtb_rep[:rows, :], ALU.add), reads=[bank, dtb_rep], writes=[dtv])
                S.op("act", lambda e: e.activation(tmp64[:rows, :], dtv[:rows, t, :], AF.Exp), reads=[dtv], writes=[tmp64])
                S.op("dve", lambda e: e.tensor_scalar(tmp64[:rows, :], tmp64[:rows, :], 1.0, None, ALU.add), reads=[tmp64], writes=[tmp64])
                S.op("act", lambda e: e.activation(dtv[:rows, t, :], tmp64[:rows, :], AF.Ln), reads=[tmp64], writes=[dtv])
                S.op("act", lambda e: e.activation(lndt[:rows, t, :], dtv[:rows, t, :], AF.Ln), reads=[dtv], writes=[lndt])
                S.op("dve", lambda e: e.tensor_tensor(av[:rows, t, :], dtv[:rows, t, :], aneg[:rows, :], ALU.mult), reads=[dtv, aneg], writes=[av])
            self.gemm_b(blk, lambda t, c: xnT[:, c, t * rows:(t + 1) * rows], [xnT], "ssd_in", j, (20,), 1, dt_ep)

            for t in range(nt):
                bank = self.psum_next()
                bv = bank[:].bitcast(BF16)
                for g in range(8):
                    S.op("pe", lambda e, g=g, t=t: e.transpose(bv[:CL, g * 128:(g + 1) * 128], BT[:, g, t * CL:(t + 1) * CL], self.ident_bf[:, :]),
                         reads=[BT, self.ident_bf], writes=[bank])
                S.op("dve", lambda e, t=t: e.tensor_copy(B_tok[:CL, t, :], bv[:CL, 0:1024]), reads=[bank], writes=[B_tok])

            if KSTOP <= 4:
                continue
            for t in range(nt):
                bankA = self.psum_next()
                S.op("dve", lambda e: e.memset(av_hi[:, :], 0.0), writes=[av_hi])
                S.op("dve", lambda e: e.memset(av_lo[:, :], 0.0), writes=[av_lo])
                S.op("dve", lambda e, t=t: e.tensor_copy(av_hi[:CL, 0:64], av[:CL, t, :]), reads=[av], writes=[av_hi])
                S.op("dve", lambda e, t=t: e.tensor_tensor(av_lo[:CL, 0:64], av[:CL, t, :], av_hi[:CL, 0:64], ALU.subtract), reads=[av, av_hi], writes=[av_lo])
                for ii, avx in enumerate((av_hi, av_lo)):
                    S.op("pe", lambda e, avx=avx, ii=ii: e.matmul(bankA[0:128, 0:CL], avx[:, :], tri_b[:, :CL], start=(ii == 0), stop=(ii == 1)), reads=[avx, tri_b], writes=[bankA])
                for ii, avx in enumerate((av_hi, av_lo)):
                    S.op("pe", lambda e, avx=avx, ii=ii: e.matmul(bankA[0:CL, 128:192], tri_b[:, :CL], avx[:, 0:64], start=(ii == 0), stop=(ii == 1)), reads=[avx, tri_b], writes=[bankA])
                for ii, avx in enumerate((av_hi, av_lo)):
                    S.op("pe", lambda e, avx=avx, ii=ii: e.matmul(bankA[0:128, 256:320], ones_b[:, 0:128], avx[:, 0:64], start=(ii == 0), stop=(ii == 1)), reads=[avx, ones_b], writes=[bankA])
                if KSTOP <= 4.2:
                    continue
                KN = int(os.environ.get('KN', '99'))
                if KN > 0:
                    S.op("dve", lambda e: e.tensor_copy(acT_f[:, 0:CL], bankA[0:128, 0:CL]), reads=[bankA], writes=[acT_f])
                if KN > 1:
                    S.op("dve", lambda e, t=t: e.tensor_copy(acT_hi[:, t, 0:CL], acT_f[:, 0:CL]), reads=[acT_f], writes=[acT_hi])
                if KN > 2:
                    S.op("dve", lambda e, t=t: e.tensor_tensor(acT_lo[:, t, 0:CL], acT_f[:, 0:CL], acT_hi[:, t, 0:CL], ALU.subtract), reads=[acT_f, acT_hi], writes=[acT_lo])
                if KN > 3:
                    S.op("dve", lambda e, t=t: e.tensor_copy(acum_sb[:CL, t, :], bankA[0:CL, 128:192]), reads=[bankA], writes=[acum_sb])
                if KN > 4:
                    S.op("act", lambda e, t=t: e.activation(e_t[:CL, t, :], acum_sb[:CL, t, :], AF.Exp), reads=[acum_sb], writes=[e_t])
                if KN > 5:
                    S.op("dve", lambda e, t=t: e.tensor_tensor(biasS[:CL, t, :], lndt[:CL, t, :], acum_sb[:CL, t, :], ALU.subtract), reads=[lndt, acum_sb], writes=[biasS])
                if KN > 6:
                    S.op("dve", lambda e, t=t: e.tensor_tensor(tmp64[:CL, :], bankA[0:CL, 256:320], biasS[:CL, t, :], ALU.add), reads=[bankA, biasS], writes=[tmp64])
                if KN > 7:
                    S.op("act", lambda e, t=t: e.activation(tailw[:CL, t, :], tmp64[:CL, :], AF.Exp), reads=[tmp64], writes=[tailw])
                if KN > 8:
                    S.op("dve", lambda e, t=t: e.tensor_copy(etot[:, t, :], bankA[0:128, 256:320]), reads=[bankA], writes=[etot]); S.op("act", lambda e, t=t: e.activation(etot[:, t, :], etot[:, t, :], AF.Exp), reads=[etot], writes=[etot])
                for g4 in range(2 if KSTOP > 4.4 else 0):
                    bank = self.psum_next()
                    for gq in range(4):
                        g = g4 * 4 + gq
                        S.op("pe", lambda e, g=g, gq=gq, t=t: e.matmul(bank[0:CL, gq * 128:gq * 128 + CL], BT[:, g, t * CL:(t + 1) * CL], CT[:, g, t * CL:(t + 1) * CL], start=True, stop=True),
                             reads=[BT, CT], writes=[bank])
                    for gq in range(4 if KSTOP > 4.6 else 0):
                        g = g4 * 4 + gq
                        S.op("dve", lambda e, g=g, gq=gq, t=t: e.tensor_tensor(cbm[:CL, t * 8 + g, 0:CL], bank[0:CL, gq * 128:gq * 128 + CL], tri[:CL, :CL], ALU.mult),
                             reads=[bank, tri], writes=[cbm])

            if KSTOP <= 5:
                continue
            for g in range(8):
                def z_ep(nb, t, bank, ncol):
                    S.op("act", lambda e: e.activation(sz_g[:rows, t, :], bank[:rows, 0:512], AF.Silu), reads=[bank], writes=[sz_g])
                self.gemm_b(blk, lambda t, c: xnT[:, c, t * rows:(t + 1) * rows], [xnT], "ssd_in", j, (g,), 1, z_ep)

                def x_ep(nb, m, bank):
                    cc = (nb - 8) * 4 + m
                    conv_ep(cc, bank, xT_stage, xT_stage[:, m, 0:T])
                self.gemm_a(blk, xnT, "ssd_in", j, (8 + g,), x_ep)
                for t in range(nt):
                    bank = self.psum_next()
                    bv = bank[:].bitcast(BF16)
                    for m in range(4):
                        S.op("pe", lambda e, m=m, t=t: e.transpose(bv[:CL, m * 128:(m + 1) * 128], xT_stage[:, m, t * CL:(t + 1) * CL], self.ident_bf[:, :]),
                             reads=[xT_stage, self.ident_bf], writes=[bank])
                    S.op("dve", lambda e, t=t: e.tensor_copy(x_tok_g[:CL, t, :], bv[:CL, 0:512]), reads=[bank], writes=[x_tok_g])

                gc = slice(g * 512, (g + 1) * 512)
                hs = slice(g * 8, (g + 1) * 8)
                for t in range(nt if KSTOP > 6 else 0):
                    k = ctr["k"]
                    ctr["k"] += 1
                    tc = slice(t * CL, (t + 1) * CL)
                    xg = x_tok_g[:CL, t, :]
                    xg3 = xg.rearrange("p (h e) -> p h e", h=8)
                    bankO = self.psum_next()
                    S.op("pe", lambda e: e.matmul(bankO[0:CL, 0:512], CT[:, g, tc], state_bf[:, gc], start=True, stop=True), reads=[CT, state_bf], writes=[bankO])
                    xd = xD[k % 2]
                    S.op("pool", lambda e: e.tensor_tensor(xd[:CL, :].rearrange("p (h e) -> p h e", h=8), xg3, D_rep[:CL, hs].unsqueeze(2).to_broadcast([CL, 8, 64]), ALU.mult),
                         reads=[x_tok_g, D_rep], writes=[xd])
                    bankD = self.psum_next()
                    S.op("pe", lambda e: e.matmul(bankD[0:CL, 0:512], self.ident_bf[:CL, :CL], xd[:CL, :], start=True, stop=False), reads=[xd, self.ident_bf], writes=[bankD])
                    Rs = []
                    bankR = None
                    for i in range(8):
                        hh = g * 8 + i
                        if i % 4 == 0:
                            bankR = self.psum_next()
                        Rsl = bankR[0:CL, (i % 4) * 128:(i % 4) * 128 + CL]
                        Rs.append((bankR, Rsl))
                        S.op("pe", lambda e, Rsl=Rsl, hh=hh: e.matmul(Rsl, onehot[:, hh, 0:CL], acT_hi[:, t, 0:CL], start=True, stop=False), reads=[onehot, acT_hi], writes=[bankR])
                        S.op("pe", lambda e, Rsl=Rsl, hh=hh: e.matmul(Rsl, onehot[:, hh, 0:CL], acT_lo[:, t, 0:CL], start=False, stop=True), reads=[onehot, acT_lo], writes=[bankR])
                    for i in range(8):
                        hh = g * 8 + i
                        bankR, Rsl = Rs[i]
                        dc = dec[i]
                        S.op("dve", lambda e, Rsl=Rsl, hh=hh, dc=dc: e.tensor_scalar(dc[:CL, :CL], Rsl, acum_sb[:CL, t, hh:hh + 1], 0.0, ALU.subtract, ALU.min), reads=[bankR, acum_sb], writes=[dc])
                    for i in range(8):
                        hh = g * 8 + i
                        dc = dec[i]
                        S.op("act", lambda e, hh=hh, dc=dc: e.activation(dc[:CL, :CL], dc[:CL, :CL], AF.Exp, bias=lndt[:CL, t, hh:hh + 1]), reads=[dc, lndt], writes=[dc])
                    for i in range(8):
                        dc = dec[i]
                        mt = MT[(k % 2) * 8 + i]
                        S.op("pool", lambda e, dc=dc, mt=mt: e.tensor_tensor(mt[:CL, :CL], dc[:CL, :CL], cbm[:CL, t * 8 + g, 0:CL], ALU.mult),
                             reads=[dc, cbm], writes=[mt])
                    for i in range(8):
                        mt = MT[(k % 2) * 8 + i]
                        S.op("pe", lambda e, mt=mt, i=i: e.matmul(bankD[0:CL, i * 64:(i + 1) * 64], mt[:CL, :CL], xg[:, i * 64:(i + 1) * 64], start=False, stop=(i == 7)),
                             reads=[mt, x_tok_g], writes=[bankD])
                    a1 = t1[k % 2]; a2 = t2[k % 2]; gb = gsb[k % 2]; gsf = gs_bf[k % 2]
                    S.op("dve", lambda e: e.tensor_tensor(a1[:CL, :].rearrange("p (h e) -> p h e", h=8), bankO[0:CL, 0:512].rearrange("p (h e) -> p h e", h=8),
                                                          e_t[:CL, t, hs].unsqueeze(2).to_broadcast([CL, 8, 64]), ALU.mult), reads=[bankO, e_t], writes=[a1])
                    S.op("dve", lambda e: e.tensor_tensor(a2[:CL, :], a1[:CL, :], bankD[0:CL, 0:512], ALU.add), reads=[a1, bankD], writes=[a2])
                    S.op("pool", lambda e: e.tensor_tensor(gb[:CL, :], a2[:CL, :], sz_g[:CL, t, :], ALU.mult), reads=[a2, sz_g], writes=[gb])
                    S.op("act", lambda e: e.activation(a1[:CL, :], gb[:CL, :], AF.Square), reads=[gb], writes=[a1])
                    kk = k % 2
                    S.op("dve", lambda e: e.reduce_sum(gss[:CL, kk:kk + 1], a1[:CL, :], axis=AX.X), reads=[a1], writes=[gss])
                    S.op("dve", lambda e: e.tensor_scalar(gss[:CL, kk:kk + 1], gss[:CL, kk:kk + 1], 1.0 / 512, EPS, ALU.mult, ALU.add), reads=[gss], writes=[gss])
                    S.op("act", lambda e: e.activation(gss[:CL, kk:kk + 1], gss[:CL, kk:kk + 1], AF.Sqrt), reads=[gss], writes=[gss])
                    S.op("dve", lambda e: e.reciprocal(gss[:CL, kk:kk + 1], gss[:CL, kk:kk + 1]), reads=[gss], writes=[gss])
                    S.op("act", lambda e: e.activation(gsf[:CL, :], gb[:CL, :], AF.Copy, scale=gss[:CL, kk:kk + 1]), reads=[gb, gss], writes=[gsf])
                    bankT = self.psum_next()
                    bvT = bankT[:].bitcast(BF16)
                    for i in range(4):
                        S.op("pe", lambda e, i=i: e.transpose(bvT[:, i * 128:i * 128 + CL], gsf[:CL, i * 128:(i + 1) * 128], self.ident_bf[:CL, :CL]),
                             reads=[gsf, self.ident_bf], writes=[bankT])
                    for i in range(4):
                        ci = g * 4 + i
                        if False:
                            pass
                        else:
                            S.op("dve", lambda e, i=i, ci=ci: e.tensor_scalar(gT[:, ci, tc], bvT[:, i * 128:i * 128 + CL], gnw[:, ci:ci + 1], None, ALU.mult), reads=[bankT, gnw], writes=[gT])
                    xwk = xw[k % 2]
                    S.op("pool", lambda e: e.tensor_tensor(xwk[:CL, :].rearrange("p (h e) -> p h e", h=8), xg3, tailw[:CL, t, hs].unsqueeze(2).to_broadcast([CL, 8, 64]), ALU.mult),
                         reads=[x_tok_g, tailw], writes=[xwk])
                    bankS = self.psum_next()
                    S.op("pe", lambda e: e.matmul(bankS[0:128, 0:512], B_tok[:CL, t, g * 128:(g + 1) * 128], xwk[:CL, :], start=True, stop=True), reads=[B_tok, xwk], writes=[bankS])
                    st3 = state[:, gc].rearrange("p (h e) -> p h e", h=8)
                    S.op("pool", lambda e: e.tensor_tensor(st3, st3, etot[:, t, hs].unsqueeze(2).to_broadcast([128, 8, 64]), ALU.mult), reads=[state, etot], writes=[state])
                    S.op("dve", lambda e: e.tensor_tensor(state[:, gc], state[:, gc], bankS[0:128, 0:512], ALU.add), reads=[state, bankS], writes=[state])
                    S.op("act", lambda e: e.activation(state_bf[:, gc], state[:, gc], AF.Copy), reads=[state], writes=[state_bf])

            ep = self.make_residual_epilogue(blk, src_idx, dst_idx, rin, rout)
            self.gemm_b(blk, lambda t, c: gT[:, c, t * rows:(t + 1) * rows], [gT], "ssd_out", j, range(4), 2, ep)

            if last:
                for jj in range(3):
                    S.op("dve", lambda e, jj=jj: e.tensor_copy(tmpc[:, :], halo[:, :, jj]), reads=[halo], writes=[tmpc])
                    bank = self.psum_next()
                    S.op("pe", lambda e: e.transpose(bank[0:48, 0:128], tmpc[:, 0:48], self.ident_f[:, :]), reads=[tmpc, self.ident_f], writes=[bank])
                    S.op("dve", lambda e: e.tensor_copy(self.rowtmp[0:48, :], bank[0:48, 0:128]), reads=[bank], writes=[self.rowtmp])
                    S.dma("sp", conv_out[jj].rearrange("(c p) -> c p", p=128), self.rowtmp[0:48, :], reads=[self.rowtmp], writes=[self.outbuf])
                dst = ssm_out.rearrange("(q r) n -> r q n", r=128)
                for qt in range(4):
                    for q4 in range(2):
                        bank = self.psum_next()
                        for qq in range(4):
                            q = qt * 8 + q4 * 4 + qq
                            S.op("pe", lambda e, q=q, qq=qq: e.transpose(bank[:, qq * 128:(qq + 1) * 128], state[:, q * 128:(q + 1) * 128], self.ident_f[:, :]),
                                 reads=[state, self.ident_f], writes=[bank])
                        S.op("dve", lambda e, q4=q4: e.tensor_copy(stload[:, q4 * 4:(q4 + 1) * 4, :], bank[:, 0:512].rearrange("p (q n) -> p q n", q=4)), reads=[bank], writes=[stload])
                    S.dma("sp", dst[:, qt * 8:(qt + 1) * 8, :], stload[:, :, :], reads=[stload], writes=[self.outbuf])
        S.barrier()


Prog.ssd_pieces = ssd_pieces
Prog.ssd_pass = ssd_pass

def att_pieces(self, j):
    pl = [self.pid[("qkv", j, 0, nb)] for nb in range(12)]
    pl += [self.pid[("wo", j, 0, nb)] for nb in range(4)]
    return pl


def att_pass(self, j, layer, src_idx, dst_idx):
    S = self.S
    din = self.din
    SEQ, PAST = self.cfg.SEQ, self.cfg.PAST
    NPT = PAST // 128
    lam_init = 0.8 - 0.6 * math.exp(-0.3 * layer)
    with ExitStack() as st:
        sb = lambda name, shape, dt: self.sb(st, name, shape, dt)
        xnT = sb("xnT", [128, 16, 512], BF16)
        hin = sb("hin", [128, D], F32)
        xs_all = sb("xs_all", [128, 4, D], BF16)
        stat = sb("stat", [128, 4], F32)
        wcol = sb("wcol", [128, 16], F32)
        qT = sb("qT", [128, 16, 512], BF16)
        NK = max(SEQ, PAST + 128)
        KT = sb("KT", [128, 2, NK], BF16)
        NKT = NK // 128
        Vext = sb("Vext", [128, NKT, 257], BF16)
        pt = [sb(f"pt{i}", [128, 512], BF16) for i in range(3)]
        o_sb = sb("o_sb", [128, 4, 256], F32)
        osq = sb("osq", [128, 256], F32)
        os_bf = [sb(f"os_bf{i}", [128, 256], BF16) for i in range(2)]
        oT = sb("oT", [128, 16, 512], BF16)
        kst = [sb(f"kst{i}", [128, 512], F32) for i in range(2)]
        kbf = [sb(f"kbf{i}", [128, 512], BF16) for i in range(2)]
        kTst = [sb(f"kTst{i}", [128, 4, 128], BF16) for i in range(2)]
        kTnew = sb("kTnew", [128, 16, DS], BF16)
        vnew = sb("vnew", [DS, 2048], BF16)
        ckst = sb("ckst", [128, max(NPT, 1), 256], BF16)
        diagb = sb("diagb", [128, AH, 128], BF16)
        augl = sb("augl", [128, AH, 128], BF16)
        augr = sb("augr", [128, self.NA], BF16)
        lq = sb("lq", [128, 256], F32)
        lk = sb("lk", [128, 256], F32)
        lsum = sb("lsum", [128, 2], F32)
        neglam = sb("neglam", [128, 1], F32)
        subcol = sb("subcol", [128, 2], F32)
        rsum = sb("rsum", [128, 4], F32)
        rs2 = sb("rs2", [128, 4], F32)
        ss = sb("ss", [128, 4], F32)
        rin = [sb(f"rin{i}", [128, 512], F32) for i in range(2)]
        rout = [sb(f"rout{i}", [128, 512], F32) for i in range(2)]

        S.dma("sp", diagb[:], din["diagb"], writes=[diagb])
        S.dma("sp", augl[:], din["augl"], writes=[augl])
        S.dma("sp", augr[:], din["augr"], writes=[augr])
        self.row_to_cols(wcol, wcol[:, 0:16], din["nmix"][layer], 16)
        self.row_to_cols(subcol, subcol[:, 0:2], din["subw"][j], 2)
        S.op("dve", lambda e: e.tensor_scalar(subcol[:, :], subcol[:, :], 1.0 - lam_init, None, ALU.mult), reads=[subcol], writes=[subcol])
        self.bcast_row(lq, din["lamq"][j], 256)
        self.bcast_row(lk, din["lamk"][j], 256)
        S.op("dve", lambda e: e.tensor_tensor(lq[:, :], lq[:, :], lk[:, :], ALU.mult), reads=[lq, lk], writes=[lq])
        S.op("dve", lambda e: e.tensor_reduce(lsum[:, 0:2], lq[:, :].rearrange("p (a b) -> p a b", a=2), axis=AX.X, op=ALU.add), reads=[lq], writes=[lsum])
        S.op("act", lambda e: e.activation(lsum[:, :], lsum[:, :], AF.Exp), reads=[lsum], writes=[lsum])
        S.op("dve", lambda e: e.tensor_tensor(neglam[:, :], lsum[:, 1:2], lsum[:, 0:1], ALU.subtract), reads=[lsum], writes=[neglam])
        S.op("dve", lambda e: e.tensor_scalar(neglam[:, :], neglam[:, :], -lam_init, None, ALU.add), reads=[neglam], writes=[neglam])
        S.op("dve", lambda e: e.memset(Vext[:, :, 256:257], 1.0), writes=[Vext])

        ctr = {"e": 0, "p": 0, "o": 0}
        for blk in self.blocks(512):
            T, rows, nt = blk.T, blk.rows, blk.nt
            isp = blk.kind == "p"
            self.convert_some(8)
            if float(os.environ.get("ASTOP", "99")) <= 0.5:
                continue
            self.prep_xnT(blk, src_idx, wcol, xnT, hin, xs_all, stat)
            if float(os.environ.get("ASTOP", "99")) <= 0.7:
                continue

            def qkv_ep(nb, t, bank, ncol):
                i = ctr["e"]
                ctr["e"] += 1
                c0 = (nb % 4) * 512
                if isp:
                    r = blk.r0 + t * 128
                    tile_buf = self.kvscr_buf[r // 128]
                if nb < 4 or nb < 8:
                    kb = kbf[i % 2]
                    kts = kTst[i % 2]
                    if nb < 4:
                        S.op("act", lambda e: e.activation(kb[:rows, :], bank[:rows, 0:512], AF.Copy, scale=float(128 ** -0.5)), reads=[bank], writes=[kb])
                    else:
                        ks = kst[i % 2]
                        S.op("dve", lambda e: e.tensor_copy(ks[:rows, :], bank[:rows, 0:512]), reads=[bank], writes=[ks])
                        S.op("act", lambda e: e.activation(kb[:rows, :], ks[:rows, :], AF.Copy), reads=[ks], writes=[kb])
                        if isp:
                            dst = self.dout["k_p"][j][r:r + 128, c0:c0 + 512]
                        else:
                            dst = self.dout["k_s"][j, blk.si][:, c0:c0 + 512]
                        S.dma("sp", dst, ks[:rows, :], reads=[ks], writes=[self.outbuf])
                    bankT = self.psum_next()
                    bvT = bankT[:].bitcast(BF16)
                    for ii in range(4):
                        S.op("pe", lambda e, ii=ii: e.transpose(bvT[:, ii * 128:ii * 128 + rows], kb[:rows, ii * 128:(ii + 1) * 128], self.ident_bf[:rows, :rows]),
                             reads=[kb, self.ident_bf], writes=[bankT])
                    src3 = bvT[:, 0:512].rearrange("p (i r) -> p i r", i=4)[:, :, 0:rows]
                    sg0 = (nb % 4) * 4
                    if nb < 4:
                        S.op("dve", lambda e: e.tensor_copy(qT[:, sg0:sg0 + 4, t * rows:(t + 1) * rows], src3), reads=[bankT], writes=[qT])
                    elif isp:
                        S.op("dve", lambda e: e.tensor_copy(kts[:, :, 0:rows], src3), reads=[bankT], writes=[kts])
                        S.dma("sp", self.kT_scr[sg0:sg0 + 4, :, r:r + 128].rearrange("s p c -> p s c"), kts[:, :, 0:rows], reads=[kts], writes=[tile_buf])
                    else:
                        S.op("dve", lambda e: e.tensor_copy(kTnew[:, sg0:sg0 + 4, 0:rows], src3), reads=[bankT], writes=[kTnew])
                else:
                    ks = kst[i % 2]
                    S.op("dve", lambda e: e.tensor_copy(ks[:rows, :], bank[:rows, 0:512]), reads=[bank], writes=[ks])
                    if isp:
                        dst = self.dout["v_p"][j][r:r + 128, c0:c0 + 512]
                    else:
                        dst = self.dout["v_s"][j, blk.si][:, c0:c0 + 512]
                    S.dma("sp", dst, ks[:rows, :], reads=[ks], writes=[self.outbuf])
                    if isp:
                        kb = kbf[i % 2]
                        S.op("act", lambda e: e.activation(kb[:rows, :], ks[:rows, :], AF.Copy), reads=[ks], writes=[kb])
                        S.dma("sp", self.v_scr[r:r + 128, c0:c0 + 512], kb[:rows, :], reads=[kb], writes=[tile_buf])
                    else:
                        S.op("act", lambda e: e.activation(vnew[:rows, c0:c0 + 512], ks[:rows, :], AF.Copy), reads=[ks], writes=[vnew])
            self.gemm_b(blk, lambda t, c: xnT[:, c, t * rows:(t + 1) * rows], [xnT], "qkv", j, range(12), 1, qkv_ep)

            ASTOP = float(os.environ.get("ASTOP", "99"))
            if ASTOP <= 1:
                continue
            if isp:
                b4 = blk.r0 // 128
                nkt = b4 + nt
                nk = nkt * 128
                kv_reads = self.kvscr_buf[0:nkt]
            for hd in range(AH):
                if isp:
                    for m in range(2):
                        S.dma("sp", KT[:, m, 0:nk], self.kT_scr[hd * 2 + m][:, 0:nk], reads=kv_reads, writes=[KT])
                    S.dma("sp", Vext[:, 0:nkt, 0:256], self.v_scr[0:nk, hd * 256:(hd + 1) * 256].rearrange("(j p) e -> p j e", p=128), reads=kv_reads, writes=[Vext])
                    tiles = []
                    for kj in range(nkt):
                        if kj < b4:
                            tiles.append((kj, 128, "aug", b4 - kj, 0))
                        else:
                            tiles.append((kj, 128, "diag", 0, kj - b4))
                else:
                    S.dma("pool", ckst[:, 0:NPT, :], din["ck"][j, blk.si][:, hd * 256:(hd + 1) * 256].rearrange("(j p) e -> p j e", p=128), reads=[], writes=[ckst])
                    S.dma("pool", Vext[:, 0:NPT, 0:256], din["cv"][j, blk.si][:, hd * 256:(hd + 1) * 256].rearrange("(j p) e -> p j e", p=128), reads=[], writes=[Vext])
                    for kj0 in range(0, NPT, 4):
                        nkk = min(4, NPT - kj0)
                        bankT = self.psum_next()
                        bvT = bankT[:].bitcast(BF16)
                        for m in range(2):
                            for q in range(nkk):
                                S.op("pe", lambda e, m=m, q=q: e.transpose(bvT[:, m * 512 + q * 128:m * 512 + (q + 1) * 128], ckst[:, kj0 + q, m * 128:(m + 1) * 128], self.ident_bf[:, :]),
                                     reads=[ckst, self.ident_bf], writes=[bankT])
                        for m in range(2):
                            S.op("dve", lambda e, m=m: e.tensor_copy(KT[:, m, kj0 * 128:(kj0 + nkk) * 128], bvT[:, m * 512:m * 512 + nkk * 128]), reads=[bankT], writes=[KT])
                    for m in range(2):
                        S.op("dve", lambda e, m=m: e.tensor_copy(KT[:, m, PAST:PAST + DS], kTnew[:, hd * 2 + m, 0:DS]), reads=[kTnew], writes=[KT])
                    S.op("dve", lambda e: e.tensor_copy(Vext[0:DS, NPT, 0:256], vnew[0:DS, hd * 256:(hd + 1) * 256]), reads=[vnew], writes=[Vext])
                    tiles = [(kj, 128, "aug", NPT - kj, 0) for kj in range(NPT)] + [(NPT, DS, "diag", 0, 0)]

                if ASTOP <= 2:
                    continue
                for m in range(2):
                    sg = hd * 2 + m
                    accs = [self.psum_next(hold=True) for _ in range(nt)]
                    def emit_qk(idx):
                        kj, kr, mode, d0, t0 = tiles[idx]
                        ncols = (nt - t0) * rows
                        bankS = self.psum_next()
                        S.op("pe", lambda e: e.matmul(bankS[0:kr, 0:ncols], KT[:, m, kj * 128:kj * 128 + kr], qT[:, sg, t0 * rows:nt * rows], start=True, stop=False),
                             reads=[KT, qT], writes=[bankS])
                        if mode == "diag":
                            S.op("pe", lambda e: e.matmul(bankS[0:kr, 0:rows], self.ident_bf[0:kr, 0:kr], diagb[0:kr, hd, 0:rows], start=False, stop=(ncols == rows)),
                                 reads=[self.ident_bf, diagb], writes=[bankS])
                            if ncols > rows:
                                S.op("pe", lambda e: e.matmul(bankS[0:kr, rows:ncols], augl[:, hd, 0:kr], augr[:, 128:128 + ncols - rows], start=False, stop=True),
                                     reads=[augl, augr], writes=[bankS])
                        else:
                            S.op("pe", lambda e: e.matmul(bankS[0:kr, 0:ncols], augl[:, hd, 0:kr], augr[:, d0 * 128:d0 * 128 + ncols], start=False, stop=True),
                                 reads=[augl, augr], writes=[bankS])
                        return bankS
                    LOOK = 2
                    pend = [emit_qk(i) for i in range(min(LOOK, len(tiles)))]
                    for idx, (kj, kr, mode, d0, t0) in enumerate(tiles):
                        ncols = (nt - t0) * rows
                        bankS = pend.pop(0)
                        if idx + LOOK < len(tiles):
                            pend.append(emit_qk(idx + LOOK))
                        p = pt[ctr["p"] % 3]
                        ctr["p"] += 1
                        S.op("act", lambda e: e.activation(p[0:kr, 0:ncols], bankS[0:kr, 0:ncols], AF.Exp), reads=[bankS], writes=[p])
                        for t in range(t0, nt if ASTOP > 3 else 0):
                            lastk = (kj == b4 + t) if isp else (idx == len(tiles) - 1)
                            S.op("pe", lambda e, t=t: e.matmul(accs[t][0:rows, 0:257], p[0:kr, (t - t0) * rows:(t - t0 + 1) * rows], Vext[0:kr, kj, 0:257], start=(idx == 0), stop=lastk),
                                 reads=[p, Vext], writes=[accs[t]])
                    for t in range(nt if ASTOP > 4 else 0):
                        S.op("dve", lambda e, t=t: e.reciprocal(rsum[:rows, t:t + 1], accs[t][0:rows, 256:257]), reads=[accs[t]], writes=[rsum])
                        if m == 0:
                            S.op("dve", lambda e, t=t: e.tensor_scalar(o_sb[:rows, t, :], accs[t][0:rows, 0:256], rsum[:rows, t:t + 1], None, ALU.mult), reads=[accs[t], rsum], writes=[o_sb])
                        else:
                            S.op("dve", lambda e, t=t: e.tensor_tensor(rs2[:rows, t:t + 1], rsum[:rows, t:t + 1], neglam[:rows, :], ALU.mult), reads=[rsum, neglam], writes=[rs2])
                            S.op("dve", lambda e, t=t: e.scalar_tensor_tensor(o_sb[:rows, t, :], accs[t][0:rows, 0:256], rs2[:rows, t:t + 1], o_sb[:rows, t, :], ALU.mult, ALU.add),
                                 reads=[accs[t], rs2, o_sb], writes=[o_sb])
                    self.psum_release(accs)
                for t in range(nt if ASTOP > 5 else 0):
                    ob = os_bf[ctr["o"] % 2]
                    ctr["o"] += 1
                    S.op("act", lambda e, t=t: e.activation(osq[:rows, :], o_sb[:rows, t, :], AF.Square), reads=[o_sb], writes=[osq])
                    S.op("dve", lambda e, t=t: e.reduce_sum(ss[:rows, t:t + 1], osq[:rows, :], axis=AX.X), reads=[osq], writes=[ss])
                    S.op("dve", lambda e, t=t: e.tensor_scalar(ss[:rows, t:t + 1], ss[:rows, t:t + 1], 1.0 / 256, EPS, ALU.mult, ALU.add), reads=[ss], writes=[ss])
                    S.op("act", lambda e, t=t: e.activation(ss[:rows, t:t + 1], ss[:rows, t:t + 1], AF.Sqrt), reads=[ss], writes=[ss])
                    S.op("dve", lambda e, t=t: e.reciprocal(ss[:rows, t:t + 1], ss[:rows, t:t + 1]), reads=[ss], writes=[ss])
                    S.op("act", lambda e, t=t: e.activation(ob[:rows, :], o_sb[:rows, t, :], AF.Copy, scale=ss[:rows, t:t + 1]), reads=[o_sb, ss], writes=[ob])
                    bankT = self.psum_next()
                    bvT = bankT[:].bitcast(BF16)
                    for half in range(2):
                        S.op("pe", lambda e, half=half: e.transpose(bvT[:, half * 128:half * 128 + rows], ob[:rows, half * 128:(half + 1) * 128], self.ident_bf[:rows, :rows]),
                             reads=[ob, self.ident_bf], writes=[bankT])
                    for half in range(2):
                        S.op("dve", lambda e, half=half, t=t: e.tensor_scalar(oT[:, hd * 2 + half, t * rows:(t + 1) * rows], bvT[:, half * 128:half * 128 + rows], subcol[:, half:half + 1], None, ALU.mult),
                             reads=[bankT, subcol], writes=[oT])

            ep = self.make_residual_epilogue(blk, src_idx, dst_idx, rin, rout)
            self.gemm_b(blk, lambda t, c: oT[:, c, t * rows:(t + 1) * rows], [oT], "wo", j, range(4), 1, ep)
        S.barrier()


Prog.att_pieces = att_pieces
Prog.att_pass = att_pass


def _add_mixers(prog):
    pass
```
